# Optimizing a Trainium2 kernel written in Bass

```python
import math
import jax, jax.numpy as jnp
from jax import lax
import numpy as np

D_MODEL = 1024
BATCH = 2
SEQ = 8192
DEPTH = 1
DEC_BATCH = 8
DEC_SEQ = 32
PAST_LEN = 1024

CHUNK = 64
N_PREV_CHUNKS = 8
BAND_CHUNKS = N_PREV_CHUNKS + 1
MIX_WIDTH = D_MODEL
A_WIDTH = MIX_WIDTH // 2
A_HEAD_DIM = 64
A_HEADS = A_WIDTH // A_HEAD_DIM
MAX_REL = 128
N_REL = 2 * MAX_REL + 1
B_WIDTH = MIX_WIDTH - A_WIDTH
B_HEAD_DIM = 64
B_HEADS = B_WIDTH // (2 * B_HEAD_DIM)
B_V_DIM = 2 * B_HEAD_DIM
ROPE_THETA = 500000.0
ROPE_DIM = B_HEAD_DIM // 4
EPS = 1e-6
Q_BLOCK = 128
IN_COLS = 4 * A_WIDTH + 4 * B_WIDTH
IN_SPLITS = tuple(int(v) for v in np.cumsum([A_WIDTH] * 4 + [B_WIDTH] * 4)[:-1])

kernel_name = "hybrid_chunkband_diffattn_stream_step"


def rmsnorm(x, g):
    xf = x.astype(jnp.float32)
    y = xf * lax.rsqrt(jnp.mean(xf * xf, axis=-1, keepdims=True) + EPS)
    return (y * g.astype(jnp.float32)).astype(x.dtype)


def partial_rope(x, pos):
    half = ROPE_DIM // 2
    inv = ROPE_THETA ** (-jnp.arange(0, ROPE_DIM, 2, dtype=jnp.float32) / ROPE_DIM)
    ang = pos.astype(jnp.float32)[:, None] * inv[None, :]
    cos = jnp.cos(ang)[:, None, None, :]
    sin = jnp.sin(ang)[:, None, None, :]
    xr = x[..., :ROPE_DIM].astype(jnp.float32)
    x1, x2 = xr[..., :half], xr[..., half:]
    rot = jnp.concatenate([x1 * cos - x2 * sin, x2 * cos + x1 * sin], axis=-1)
    return jnp.concatenate([rot.astype(x.dtype), x[..., ROPE_DIM:]], axis=-1)


def split_proj(h, w_in):
    z = jnp.einsum('bsd,dc->bsc', h, w_in)
    b, s, _ = z.shape
    qa, ka, va, ga, qb, kb, vb, gb = jnp.split(z, IN_SPLITS, axis=-1)
    qa = qa.reshape(b, s, A_HEADS, A_HEAD_DIM)
    ka = ka.reshape(b, s, A_HEADS, A_HEAD_DIM)
    va = va.reshape(b, s, A_HEADS, A_HEAD_DIM)
    qb = qb.reshape(b, s, B_HEADS, 2, B_HEAD_DIM)
    kb = kb.reshape(b, s, B_HEADS, 2, B_HEAD_DIM)
    vb = vb.reshape(b, s, B_HEADS, B_V_DIM)
    return qa, ka, va, ga, qb, kb, vb, gb


def rel_index(rel):
    return jnp.clip(rel, -MAX_REL, MAX_REL) + MAX_REL


def band_attention_prompt(q, k, v, rel_bias):
    b, s, h, d = q.shape
    nc = s // CHUNK
    pad = N_PREV_CHUNKS * CHUNK
    kp = jnp.pad(k, ((0, 0), (pad, 0), (0, 0), (0, 0))).reshape(b, nc + N_PREV_CHUNKS, CHUNK, h, d)
    vp = jnp.pad(v, ((0, 0), (pad, 0), (0, 0), (0, 0))).reshape(b, nc + N_PREV_CHUNKS, CHUNK, h, d)
    kband = jnp.concatenate([kp[:, j:j + nc] for j in range(BAND_CHUNKS)], axis=2)
    vband = jnp.concatenate([vp[:, j:j + nc] for j in range(BAND_CHUNKS)], axis=2)
    qc = q.reshape(b, nc, CHUNK, h, d)
    scores = jnp.einsum('bcqhd,bckhd->bchqk', qc, kband,
                        preferred_element_type=jnp.float32) * (d ** -0.5)
    band_len = BAND_CHUNKS * CHUNK
    i = jnp.arange(CHUNK)
    j = jnp.arange(band_len)
    rel = i[:, None] - j[None, :] + pad
    bias = rel_bias.astype(jnp.float32)[:, rel_index(rel)]
    k_pos = jnp.arange(nc)[:, None] * CHUNK - pad + j[None, :]
    valid = (k_pos >= 0)[None, :, None, None, :]
    scores = jnp.where(valid, scores + bias[None, None], -jnp.inf)
    p = jax.nn.softmax(scores, axis=-1)
    out = jnp.einsum('bchqk,bckhd->bcqhd', p.astype(v.dtype), vband)
    return out.reshape(b, s, h * d)


def band_attention_sample(q, k_new, v_new, k_cache, v_cache, rel_bias):
    b, t, h, d = q.shape
    L = k_cache.shape[1]
    keys = jnp.concatenate([k_cache.astype(k_new.dtype), k_new], axis=1)
    vals = jnp.concatenate([v_cache.astype(v_new.dtype), v_new], axis=1)
    scores = jnp.einsum('bqhd,bkhd->bhqk', q, keys,
                        preferred_element_type=jnp.float32) * (d ** -0.5)
    rel = (L + jnp.arange(t))[:, None] - jnp.arange(L + t)[None, :]
    bias = rel_bias.astype(jnp.float32)[:, rel_index(rel)]
    p = jax.nn.softmax(scores + bias[None], axis=-1)
    out = jnp.einsum('bhqk,bkhd->bqhd', p.astype(vals.dtype), vals).reshape(b, t, h * d)
    return out, keys[:, -L:], vals[:, -L:]


def diff_lambda(lq1, lk1, lq2, lk2, lambda_init):
    f32 = lambda a: a.astype(jnp.float32)
    return (jnp.exp(jnp.sum(f32(lq1) * f32(lk1))) - jnp.exp(jnp.sum(f32(lq2) * f32(lk2)))
            + lambda_init)


def diff_core(q, k, v, mask, lam):
    d = q.shape[-1]
    scores = jnp.einsum('bqhmd,bkhmd->bhmqk', q, k,
                        preferred_element_type=jnp.float32) * (d ** -0.5)
    scores = jnp.where(mask, scores, -jnp.inf)
    p = jax.nn.softmax(scores, axis=-1)
    attn = p[:, :, 0] - lam * p[:, :, 1]
    return jnp.einsum('bhqk,bkhe->bqhe', attn.astype(v.dtype), v)


def diff_attention_prompt(q, k, v, lam):
    b, s, h, _, d = q.shape
    nblk = s // Q_BLOCK
    qblocks = q.reshape(b, nblk, Q_BLOCK, h, 2, d).transpose(1, 0, 2, 3, 4, 5)
    k_chunk = jnp.arange(s) // CHUNK

    def one_block(args):
        qblk, idx = args
        q_chunk = (idx * Q_BLOCK + jnp.arange(Q_BLOCK)) // CHUNK
        mask = k_chunk[None, :] <= q_chunk[:, None]
        return diff_core(qblk, k, v, mask, lam)

    out = lax.map(one_block, (qblocks, jnp.arange(nblk)))
    return out.transpose(1, 0, 2, 3, 4).reshape(b, s, h, v.shape[-1])


def diff_attention_sample(q, k_new, v_new, k_cache, v_cache, lam):
    t = q.shape[1]
    keys = jnp.concatenate([k_cache.astype(k_new.dtype), k_new], axis=1)
    vals = jnp.concatenate([v_cache.astype(v_new.dtype), v_new], axis=1)
    mask = jnp.ones((t, keys.shape[1]), dtype=bool)
    return diff_core(q, keys, vals, mask, lam)


def diff_subnorm(o, g, lambda_init):
    b, s = o.shape[0], o.shape[1]
    return (rmsnorm(o, g) * (1.0 - lambda_init)).reshape(b, s, B_WIDTH)


def merge_out(oa, ga, ob, gb, w_out):
    o = jnp.concatenate([oa * jax.nn.silu(ga), ob * jax.nn.silu(gb)], axis=-1)
    return jnp.einsum('bsc,cd->bsd', o, w_out)


def setup_inputs(seed: int = 0) -> dict:
    key = jax.random.key(seed)
    ks = jax.random.split(key, 17)
    a_keep = min(N_PREV_CHUNKS * CHUNK, PAST_LEN)
    nrm = lambda k, shape, sc: jax.random.normal(k, shape, jnp.float32) * sc
    return {
        'x_prompt': nrm(ks[0], (BATCH, SEQ, D_MODEL), 1.0),
        'x_sample': nrm(ks[1], (DEC_BATCH, DEC_SEQ, D_MODEL), 1.0),
        'cache_a_k': nrm(ks[2], (DEPTH, DEC_BATCH, a_keep, A_HEADS, A_HEAD_DIM), 1.0),
        'cache_a_v': nrm(ks[3], (DEPTH, DEC_BATCH, a_keep, A_HEADS, A_HEAD_DIM), 1.0),
        'cache_b_k': nrm(ks[4], (DEPTH, DEC_BATCH, PAST_LEN, B_HEADS, 2, B_HEAD_DIM), 1.0),
        'cache_b_v': nrm(ks[5], (DEPTH, DEC_BATCH, PAST_LEN, B_HEADS, B_V_DIM), 1.0),
        'norm_gain': 1.0 + nrm(ks[6], (DEPTH, D_MODEL), 0.05),
        'w_in': nrm(ks[7], (DEPTH, D_MODEL, IN_COLS), D_MODEL ** -0.5),
        'w_out': nrm(ks[8], (DEPTH, MIX_WIDTH, D_MODEL), MIX_WIDTH ** -0.5),
        'rel_bias': nrm(ks[9], (DEPTH, A_HEADS, N_REL), 0.1),
        'lambda_q1': nrm(ks[10], (DEPTH, B_HEAD_DIM), 0.1),
        'lambda_k1': nrm(ks[11], (DEPTH, B_HEAD_DIM), 0.1),
        'lambda_q2': nrm(ks[12], (DEPTH, B_HEAD_DIM), 0.1),
        'lambda_k2': nrm(ks[13], (DEPTH, B_HEAD_DIM), 0.1),
        'subln_gain': 1.0 + nrm(ks[14], (DEPTH, B_V_DIM), 0.05),
        'final_gain': 1.0 + nrm(ks[15], (D_MODEL,), 0.05),
    }


def reference(x_prompt, x_sample, cache_a_k, cache_a_v, cache_b_k, cache_b_v,
              norm_gain, w_in, w_out, rel_bias, lambda_q1, lambda_k1, lambda_q2, lambda_k2,
              subln_gain, final_gain):
    s_prompt = x_prompt.shape[1]
    t_sample = x_sample.shape[1]
    past_len = cache_b_k.shape[2]
    a_keep_prompt = min(N_PREV_CHUNKS * CHUNK, s_prompt)
    pos_prompt = jnp.arange(s_prompt)
    pos_sample = past_len + jnp.arange(t_sample)
    yp, ys = x_prompt, x_sample
    akp, avp, bkp, bvp, aks, avs, bks, bvs = [], [], [], [], [], [], [], []
    for l in range(DEPTH):
        lambda_init = 0.8 - 0.6 * math.exp(-0.3 * l)
        lam = diff_lambda(lambda_q1[l], lambda_k1[l], lambda_q2[l], lambda_k2[l], lambda_init)
        h = rmsnorm(yp, norm_gain[l])
        qa, ka, va, ga, qb, kb, vb, gb = split_proj(h, w_in[l])
        qb = partial_rope(qb, pos_prompt)
        kb = partial_rope(kb, pos_prompt)
        oa = band_attention_prompt(qa, ka, va, rel_bias[l])
        ob = diff_subnorm(diff_attention_prompt(qb, kb, vb, lam), subln_gain[l], lambda_init)
        yp = yp + merge_out(oa, ga, ob, gb, w_out[l])
        akp.append(ka[:, s_prompt - a_keep_prompt:])
        avp.append(va[:, s_prompt - a_keep_prompt:])
        bkp.append(kb)
        bvp.append(vb)
        h = rmsnorm(ys, norm_gain[l])
        qa, ka, va, ga, qb, kb, vb, gb = split_proj(h, w_in[l])
        qb = partial_rope(qb, pos_sample)
        kb = partial_rope(kb, pos_sample)
        oa, ka_buf, va_buf = band_attention_sample(qa, ka, va, cache_a_k[l], cache_a_v[l], rel_bias[l])
        ob = diff_subnorm(diff_attention_sample(qb, kb, vb, cache_b_k[l], cache_b_v[l], lam),
                          subln_gain[l], lambda_init)
        ys = ys + merge_out(oa, ga, ob, gb, w_out[l])
        aks.append(ka_buf)
        avs.append(va_buf)
        bks.append(kb)
        bvs.append(vb)
    y_prompt = rmsnorm(yp, final_gain)
    y_sample = rmsnorm(ys, final_gain)
    return (y_prompt, y_sample,
            jnp.stack(akp), jnp.stack(avp), jnp.stack(bkp), jnp.stack(bvp),
            jnp.stack(aks), jnp.stack(avs), jnp.stack(bks), jnp.stack(bvs))
```

```python
import math
import numpy as np
import ml_dtypes
from contextlib import ExitStack

import concourse.bass as bass
import concourse.mybir as mybir
from concourse.bass_utils import run_bass_kernel_spmd

F32 = mybir.dt.float32
BF16 = mybir.dt.bfloat16
AF = mybir.ActivationFunctionType
ALU = mybir.AluOpType

S = 8192
NPT = 64
NT = 68
NTOK = 8320
EPS = 1e-6
LAMBDA_INIT = 0.8 - 0.6 * math.exp(0.0)
NEG = 0.0


class Buf:
    __slots__ = ("name", "last_w", "readers", "bank")

    def __init__(self, name, bank=False):
        self.name = name
        self.last_w = None
        self.readers = []
        self.bank = bank


class Op:
    __slots__ = ("eng", "fn", "deps", "needed", "inc_idx", "akey", "aval", "aamt")

    def __init__(self, eng, fn):
        self.eng = eng
        self.fn = fn
        self.deps = []
        self.needed = False
        self.inc_idx = None
        self.akey = None
        self.aval = None
        self.aamt = 16


class Prog:
    ENGS = ("pe", "act", "dve", "pool", "sp")

    def __init__(self):
        self.ops = []
        self.acount = {}
        self.group_keys = set()

    def op(self, eng, fn, reads=(), writes=(), akey=None, aamt=16, group=False):
        o = Op(eng, fn)
        deps = []
        for b in reads:
            if b.last_w is not None:
                deps.append(b.last_w)
        for b in writes:
            if b.bank:
                deps.extend(r for r in b.readers if r.eng != eng)
                continue
            if b.last_w is not None:
                deps.append(b.last_w)
            deps.extend(b.readers)
        seen = set()
        for d in deps:
            if id(d) in seen or d is o:
                continue
            seen.add(id(d))
            if d.akey is None and d.eng == eng and eng == "pe":
                continue
            o.deps.append(d)
            d.needed = True
        for b in reads:
            b.readers.append(o)
        for b in writes:
            if b.bank:
                b.readers = [r for r in b.readers if r.eng != eng] + [o]
                continue
            b.last_w = o
            b.readers = []
        if akey is not None:
            o.akey = akey
            o.aamt = aamt
            self.acount[akey] = self.acount.get(akey, 0) + aamt
            o.aval = self.acount[akey]
            if group:
                self.group_keys.add(akey)
        self.ops.append(o)
        return o

    def emit(self, nc, block, stack):
        cnt = {e: 0 for e in self.ENGS}
        for o in self.ops:
            if o.akey is None and o.needed:
                cnt[o.eng] += 1
                o.inc_idx = cnt[o.eng]
        sems = {e: stack.enter_context(nc.semaphore("s_" + e)) for e in self.ENGS}
        asems = {k: stack.enter_context(nc.semaphore("a_" + k)) for k in self.acount}
        ops = self.ops
        acount = self.acount
        group_keys = self.group_keys

        def run(engname, h):
            waited = {}
            for o in ops:
                if o.eng != engname:
                    continue
                for d in o.deps:
                    if d.akey is not None:
                        sem = asems[d.akey]
                        val = acount[d.akey] if d.akey in group_keys else d.aval
                        key = "a_" + d.akey
                    else:
                        sem = sems[d.eng]
                        val = d.inc_idx
                        key = d.eng
                    if waited.get(key, 0) >= val:
                        continue
                    waited[key] = val
                    h.wait_ge(sem, val)
                ins = o.fn(h)
                if o.akey is not None:
                    ins.then_inc(asems[o.akey], o.aamt)
                elif o.needed:
                    ins.then_inc(sems[o.eng], 1)
            if engname == "sp":
                for k, tot in acount.items():
                    h.wait_ge(asems[k], tot)

        @block.tensor
        def _(h):
            run("pe", h)

        @block.scalar
        def _(h):
            run("act", h)

        @block.vector
        def _(h):
            run("dve", h)

        @block.gpsimd
        def _(h):
            run("pool", h)

        @block.sync
        def _(h):
            run("sp", h)


def build_nc(cfg=None):
    cfg = cfg or {}
    NTR = cfg.get('ntiles', NT)
    DO_CC = cfg.get('cc', True)
    DO_P2 = cfg.get('p2', True)
    DO_ROLL = cfg.get('roll', True)
    nc = bass.Bass("TRN2", target_bir_lowering=False)
    P = Prog()

    def din(name, shape, dt=F32):
        return nc.dram_tensor(name, shape, dt, kind="ExternalInput")

    def dout(name, shape, dt=F32):
        return nc.dram_tensor(name, shape, dt, kind="ExternalOutput")

    x_all = din("x_all", [NTOK, 1024])
    w_in = din("w_in", [8, 128, 1024])
    gain_t = din("gain_t", [128, 8])
    w_out = din("w_out", [8, 128, 256])
    sgain = din("sgain", [128, 1])
    xres = din("xres", [NTOK, 256])
    fgain = din("fgain", [1, 256])
    relb = din("relb", [2, 257])
    lam4 = din("lam4", [1, 256])
    cst = din("cst", [NTOK, 64])
    c_ak = din("c_ak", [4, 512, 128])
    c_av = din("c_av", [4, 512, 128])
    c_bk = din("c_bk", [4, 1024, 128])
    c_bv = din("c_bv", [4, 1024, 128])
    ident_d = din("ident", [128, 128])
    jmat_d = din("jmat", [128, 128])

    y_out = dout("y_out", [NTOK, 256])
    kb_out = dout("kb_out", [NTOK, 128])
    vb_out = dout("vb_out", [NTOK, 128])
    akp_out = dout("akp_out", [512, 128])
    avp_out = dout("avp_out", [512, 128])
    aks_out = dout("aks_out", [4, 512, 128])
    avs_out = dout("avs_out", [4, 512, 128])

    ext_d = nc.dram_tensor("ext_d", [2, 768], F32)
    CH_TOK = [1536] * 5 + [640]
    o_bounce = [nc.dram_tensor("o_bounce%d" % c, [256, CH_TOK[c]], BF16) for c in range(6)]
    o_all = [nc.dram_tensor("o_all%d" % c, [1024, CH_TOK[c]], BF16) for c in range(6)]

    def chunk_of(grp):
        if grp < 15:
            return grp // 3, (grp % 3) * 512
        return 5, (grp - 15) * 512
    CH_TILES = [12] * 5 + [5]
    ss_b = [nc.dram_tensor("ss_b%d" % c, [128, CH_TILES[c]], F32) for c in range(6)]
    ss_a = [nc.dram_tensor("ss_a%d" % c, [128, CH_TILES[c]], F32) for c in range(6)]
    y_scr = nc.dram_tensor("y_scr", [NTOK, 256], F32)

    st = ExitStack()
    with st:
        def sb(name, shape, dt=F32):
            return st.enter_context(nc.sbuf_tensor("sb_" + name, shape, dt))

        ps = st.enter_context(nc.psum_tensor("ps", [128, 8, 512], F32))

        ident_f = sb("ident_f", [128, 128])
        ident_b = sb("ident_b", [128, 128], BF16)
        jmat = sb("jmat", [128, 128])
        jmat_b = sb("jmat_b", [128, 128], BF16)
        Wg = sb("Wg", [128, 8, 1024], BF16)
        Wo = sb("Wo", [128, 8, 256], BF16)
        gt = sb("gt", [128, 8])
        sg = sb("sg", [128, 1])
        sg8 = sb("sg8", [128, 1])
        fgb = sb("fgb", [128, 256])
        lamv = sb("lamv", [128, 256])
        lsc = sb("lsc", [128, 8])
        neghalf = sb("neghalf", [128, 68])
        rb2 = sb("rb2", [2, 257])
        Hst = sb("Hst", [128, 2, 5, 128])
        EBf = sb("EBf", [128, 2, 5, 128])
        EBu = sb("EBu", [128, 2, 5, 128], BF16)
        EBm = sb("EBm", [128, 2, 5, 128], BF16)

        xt = sb("xt", [128, 3, 1024])
        cs_t = sb("cs_t", [128, 4, 64])
        xb = sb("xb", [128, 3, 1024], BF16)
        ssx = sb("ssx", [128, NT])
        xsq = sb("xsq", [128, 1024], BF16)
        hT = sb("hT", [128, 2, 8, 128], BF16)
        stt = sb("stt", [128, 2, 2, 6])
        mv = sb("mv", [128, 2, 2])
        msq = sb("msq", [128, 2, 2])
        rstd_all = sb("rstd_all", [128, NT])
        hr_all = sb("hr_all", [128, NT])
        qk_bf = sb("qk_bf", [128, 2, 512], BF16)
        rb = sb("rb", [128, 2, 256])
        rtmp = sb("rtmp", [128, 2, 4, 32])
        vstage = sb("vstage", [128, 2, 256])
        kastage = sb("kastage", [128, 2, 128])
        th = sb("th", [128, 2, 256])
        G = sb("G", [128, 4, 256])
        Gs = sb("Gs", [128, 4, 256])
        K2 = sb("K2", [128, 2, NTOK], BF16)
        QA = sb("QA", [128, 2, 2, 128], BF16)
        QB = sb("QB", [128, 2, 512], BF16)
        QBs = sb("QBs", [128, 4, 32], BF16)
        Va = sb("Va", [128, 12, 2, 65], BF16)
        Vb = sb("Vb", [128, NT, 129], BF16)
        E_A = sb("E_A", [128, 2, 2, 5, 128], BF16)
        PB = sb("PB", [128, 4, 2, 288], BF16)
        OG = sb("OG", [128, 4, 256], BF16)
        OGs = sb("OGs", [128, 4, 256], BF16)
        OGT = sb("OGT", [128, 2, 2, 512], BF16)
        epi = sb("epi", [128, 4, 8])
        t1 = sb("t1", [128, 2, 128])
        dd = sb("dd", [128, 2, 128])
        junk = sb("junk", [128, 256])
        ck16a = sb("ck16a", [128, 2, 4, 128], BF16)
        ck16b = sb("ck16b", [128, 2, 8, 128], BF16)
        CKA = sb("CKA", [128, 512], BF16)
        CVA = sb("CVA", [128, 4, 2, 65], BF16)
        CKB = sb("CKB", [128, 1024], BF16)
        CVB = sb("CVB", [128, 8, 129], BF16)
        oT = sb("oT", [128, 2, 8, 512], BF16)
        xr = sb("xr", [128, 3, 256])
        yt = sb("yt", [128, 3, 256])
        ssq = sb("ssq", [128, 65])
        accS = sb("accS", [128, 2, 2, 129])
        ext_sb = xr[0:2, :, :].rearrange("p a c -> p (a c)")
        zer = yt[0:2, :, :].rearrange("p a c -> p (a c)")[:, 0:512]
        fst = sb("fst", [128, 2, 1024])
        ssr = sb("ssr", [128, 65])
        rstdf = sb("rstdf", [128, 65])

        bufs = {}

        def B(name):
            b = bufs.get(name)
            if b is None:
                b = Buf(name)
                bufs[name] = b
            return b

        banks = [Buf("bank%d" % i, bank=True) for i in range(8)]

        def BK(i):
            return banks[i]

        def dma(out, in_, reads, writes, key, eng="sp", group=False):
            return P.op(eng, lambda h, out=out, in_=in_: h.dma_start(out=out, in_=in_),
                        reads=reads, writes=writes, akey=key, group=group)

        def pe(fn, reads, writes):
            return P.op("pe", fn, reads, writes)

        def act(fn, reads, writes):
            return P.op("act", fn, reads, writes)

        def dve(fn, reads, writes):
            return P.op("dve", fn, reads, writes)

        def pool(fn, reads, writes):
            return P.op("pool", fn, reads, writes)

        def mm(out, lhsT, rhs, start, reads, writes):
            return pe(lambda h: h.matmul(out, lhsT=lhsT, rhs=rhs, start=start, stop=True,
                                         skip_group_check=True), reads, writes)

        def tr(out, in_, idn, reads, writes):
            return pe(lambda h: h.transpose(out, in_, idn), reads, writes)

        def bank_bf(b, lo=0, hi=512):
            return ps[:, b, lo:hi].bitcast(BF16)

        tp_v = bank_bf(4).rearrange("p (k t) -> p k t", k=8)
        tq_v = bank_bf(4, 0, 256).rearrange("p (k t) -> p k t", k=4)
        og_v = bank_bf(7, 0, 128).rearrange("p (k t) -> p k t", k=2)
        accA = ps[:, 6, 256:386].rearrange("p (h d) -> p h d", h=2)
        accB = [ps[:, 2, 0:258].rearrange("p (s d) -> p s d", s=2),
                ps[:, 3, 0:258].rearrange("p (s d) -> p s d", s=2)]

        SET = "setup"
        dma(ident_f[:], ident_d.ap(), [], [B("ident_f")], SET, group=True)
        dma(jmat[:], jmat_d.ap(), [], [B("jmat")], SET, group=True)
        dma(gt[:], gain_t.ap(), [], [B("gt")], SET, group=True)
        dma(sg[:], sgain.ap(), [], [B("sg")], SET, group=True)
        dma(fgb[:], fgain.ap().partition_broadcast(128).rearrange("p a c -> p (a c)") if False else
            bass.AP(tensor=fgain, offset=0, ap=[[0, 128], [1, 256]]), [], [B("fgb")], SET, group=True)
        dma(lamv[:], bass.AP(tensor=lam4, offset=0, ap=[[0, 128], [1, 256]]), [], [B("lamv")], SET, group=True)
        dma(rb2[:], relb.ap(), [], [B("rb2")], SET, group=True)
        dve(lambda h: h.tensor_copy(out=ident_b[:], in_=ident_f[:]), [B("ident_f")], [B("ident_b")])
        pool(lambda h: h.memset(neghalf[:], -0.5), [], [B("neghalf")])
        pool(lambda h: h.memset(zer[:], 0.0), [], [B("zer")])
        pool(lambda h: h.memset(Vb[:, :, 128:129], 1.0), [], [B("Vb_ones")])
        pool(lambda h: h.memset(Va[:, :, :, 64:65], 1.0), [], [B("Va_ones")])
        pool(lambda h: h.memset(CVA[:, :, :, 64:65], 1.0), [], [B("CVA_ones")])
        pool(lambda h: h.memset(CVB[:, :, 128:129], 1.0), [], [B("CVB_ones")])
        dve(lambda h: h.tensor_scalar(out=sg8[:], in0=sg[:], scalar1=float(1.0 - LAMBDA_INIT), scalar2=None,
                                      op0=ALU.mult), [B("sg")], [B("sg8")])

        STG = cfg.get('stage', 99)
        for kc in range(8 if STG >= 1 else 0):
            s_ = kc % 2
            dma(xt[:, s_, :], w_in.ap()[kc], [], [B("xt%d" % s_)], "xt%d" % s_)
            dve(lambda h, kc=kc, s_=s_: h.tensor_scalar(out=Wg[:, kc, :], in0=xt[:, s_, :],
                                                        scalar1=gt[:, kc:kc + 1], scalar2=None, op0=ALU.mult),
                [B("xt%d" % s_), B("gt")], [B("Wg")])
        for cc in range(8 if STG >= 2 else 0):
            s_ = cc % 2
            dma(xt[:, s_, 0:256], w_out.ap()[cc], [], [B("xt%d" % s_)], "xt%d" % s_)
            if cc % 2 == 1:
                dve(lambda h, cc=cc, s_=s_: h.tensor_scalar(out=Wo[:, cc, :], in0=xt[:, s_, 0:256],
                                                            scalar1=sg8[:, 0:1], scalar2=None, op0=ALU.mult),
                    [B("xt%d" % s_), B("sg8")], [B("Wo")])
            else:
                dve(lambda h, cc=cc, s_=s_: h.tensor_copy(out=Wo[:, cc, :], in_=xt[:, s_, 0:256]),
                    [B("xt%d" % s_)], [B("Wo")])

        if STG >= 3:
            dve(lambda h: h.tensor_tensor(out=junk[:, 0:64], in0=lamv[:, 0:64], in1=lamv[:, 64:128], op=ALU.mult),
                [B("lamv")], [B("junk")])
            dve(lambda h: h.tensor_reduce(out=lsc[:, 0:1], in_=junk[:, 0:64], axis=mybir.AxisListType.X, op=ALU.add),
                [B("junk")], [B("lsc0")])
            dve(lambda h: h.tensor_tensor(out=junk[:, 64:128], in0=lamv[:, 128:192], in1=lamv[:, 192:256], op=ALU.mult),
                [B("lamv")], [B("junkb")])
            dve(lambda h: h.tensor_reduce(out=lsc[:, 1:2], in_=junk[:, 64:128], axis=mybir.AxisListType.X, op=ALU.add),
                [B("junkb")], [B("lsc1")])
            act(lambda h: h.activation(out=lsc[:, 2:4], in_=lsc[:, 0:2], func=AF.Exp), [B("lsc0"), B("lsc1")], [B("lsc2")])
            dve(lambda h: h.scalar_tensor_tensor(out=lsc[:, 4:5], in0=lsc[:, 3:4], scalar=float(-LAMBDA_INIT),
                                                 in1=lsc[:, 2:3], op0=ALU.add, op1=ALU.subtract),
                [B("lsc2")], [B("neglam")])

        if STG >= 4:
            dve(lambda h: h.tensor_copy(out=ext_sb[:, 0:256], in_=rb2[:, 1:257]), [B("rb2")], [B("ext_sb")])
            dve(lambda h: h.tensor_scalar(out=ext_sb[:, 256:768], in0=zer[:, 0:512], scalar1=rb2[:, 256:257],
                                          scalar2=None, op0=ALU.add), [B("rb2"), B("zer")], [B("ext_sb")])
            dma(ext_d.ap(), ext_sb[:], [B("ext_sb")], [B("ext_d")], "ext")
            dma(Hst[:], bass.AP(tensor=ext_d, offset=0, ap=[[1, 128], [768, 2], [128, 5], [1, 128]]),
                [B("ext_d")], [B("Hst")], "hst")
            Hflat = Hst[:].rearrange("p h r q -> p (h r q)")
            EBflat = EBf[:].rearrange("p h r q -> p (h r q)")
            Hhi = OGT[:].rearrange("p a b t -> p (a b t)")[:, 0:1280]
            Hlo = E_A[:].rearrange("p a h r q -> p (a h r q)")[:, 0:1280]
            if STG >= 5:
                dve(lambda h: h.tensor_copy(out=jmat_b[:], in_=jmat[:]), [B("jmat")], [B("jmat_b")])
                dve(lambda h: h.tensor_copy(out=Hhi, in_=Hflat), [B("Hst")], [B("Hhi")])
                dve(lambda h: h.tensor_tensor(out=EBflat, in0=Hflat, in1=Hhi, op=ALU.subtract),
                    [B("Hst"), B("Hhi")], [B("EBf")])
                dve(lambda h: h.tensor_copy(out=Hlo, in_=EBflat), [B("EBf")], [B("Hlo")])
                for i, (c0, c1) in enumerate(((0, 512), (512, 1024), (1024, 1280))):
                    mm(ps[:, 6, 0:c1 - c0], jmat_b[:], Hhi[:, c0:c1], True, [B("jmat_b"), B("Hhi")], [BK(6)])
                    mm(ps[:, 6, 0:c1 - c0], jmat_b[:], Hlo[:, c0:c1], False, [B("jmat_b"), B("Hlo")], [BK(6)])
                    dve(lambda h, c0=c0, c1=c1: h.tensor_copy(out=EBflat[:, c0:c1], in_=ps[:, 6, 0:c1 - c0]),
                        [], [B("EBf"), BK(6)])
            if STG >= 6:
                dve(lambda h: h.tensor_scalar(out=EBu[:], in0=EBf[:], scalar1=8.0, scalar2=None, op0=ALU.mult),
                    [B("EBf")], [B("EBu")])
                dve(lambda h: h.tensor_copy(out=EBm[:], in_=EBu[:]), [B("EBu")], [B("EBm")])
                pool(lambda h: h.memset(EBm[64:128, :, 0, 0:64], -30000.0), [B("EBm")], [B("EBm")])
                pool(lambda h: h.memset(EBm[0:64, :, 4, 64:128], -30000.0), [B("EBm")], [B("EBm")])

        if NTR >= NT:
            tiles = list(range(NPT, NT)) + list(range(NPT))
        else:
            tiles = list(range(NTR))
        pos = {t: i for i, t in enumerate(tiles)}

        def seq_next(t, k):
            i = pos[t] + k
            return tiles[i] if i < len(tiles) else None

        def S2(t):
            return pos[t] % 2

        def S3(t):
            return pos[t] % 3

        def pbank(t):
            return 4 + pos[t] % 2

        def abank(t):
            return 6 + pos[t] % 2

        def tpv(b):
            return bank_bf(b).rearrange("p (k t) -> p k t", k=8)

        def tqv(b):
            return bank_bf(b, 0, 256).rearrange("p (k t) -> p k t", k=4)

        def ogv(b):
            return bank_bf(b, 0, 128).rearrange("p (k t) -> p k t", k=2)

        def accAv(b):
            return ps[:, b, 256:386].rearrange("p (h d) -> p h d", h=2)

        def va_idx(t):
            return (t % 8) if t < NPT else 8 + (t - NPT)

        def tile_geo(t):
            if t < NPT:
                return 128, 128 * t
            return 32, S + 32 * (t - NPT)

        def load_x(t):
            if t is None:
                return
            nt, r0 = tile_geo(t)
            s3 = S3(t)
            dma(xt[:nt, s3, :], x_all.ap()[r0:r0 + nt, :], [], [B("xt%d" % s3)], "xt%d" % s3)
            s4 = pos[t] % 4
            dma(cs_t[:nt, s4, :], cst.ap()[r0:r0 + nt, :], [], [B("cs%d" % s4)], "cs%d" % s4)
            dma(xb[:nt, s3, :], x_all.ap()[r0:r0 + nt, :], [], [B("xb%d" % s3)], "xb%d" % s3, eng="pool")

        ACT_TILES = cfg.get('act_tiles', 68)

        def prep(t):
            if t is None:
                return
            nt, r0 = tile_geo(t)
            s3 = S3(t)
            s_ = S2(t)
            bx = B("xt%d" % s3)
            if t < ACT_TILES or t >= NPT:
                act(lambda h: h.activation(out=junk[:nt, 0:1024] if False else xsq[:nt, :], in_=xt[:nt, s3, :],
                                           func=AF.Square, accum_out=ssx[:nt, t:t + 1]), [bx], [B("xsq"), B("ssx%d" % t)])
                dve(lambda h: h.tensor_scalar(out=msq[:nt, s_, 1:2], in0=ssx[:nt, t:t + 1], scalar1=1.0 / 1024.0,
                                              scalar2=float(EPS), op0=ALU.mult, op1=ALU.add),
                    [B("ssx%d" % t)], [B("msq%db" % s_)])
            else:
                dve(lambda h: h.bn_stats(out=stt[:nt, s_, 0, :], in_=xt[:nt, s3, 0:512]), [bx], [B("stt%d" % s_)])
                dve(lambda h: h.bn_stats(out=stt[:nt, s_, 1, :], in_=xt[:nt, s3, 512:1024]), [bx], [B("stt%d" % s_)])
                dve(lambda h: h.bn_aggr(out=mv[:nt, s_, :], in_=stt[:nt, s_, :, :].rearrange("p a b -> p (a b)")),
                    [B("stt%d" % s_)], [B("mv%d" % s_)])
                dve(lambda h: h.scalar_tensor_tensor(out=msq[:nt, s_, 0:1], in0=mv[:nt, s_, 0:1],
                                                     scalar=mv[:nt, s_, 0:1], in1=mv[:nt, s_, 1:2], op0=ALU.mult,
                                                     op1=ALU.add), [B("mv%d" % s_)], [B("msq%da" % s_)])
                dve(lambda h: h.tensor_scalar(out=msq[:nt, s_, 1:2], in0=msq[:nt, s_, 0:1], scalar1=float(EPS),
                                              scalar2=None, op0=ALU.add), [B("msq%da" % s_)], [B("msq%db" % s_)])
            pool(lambda h: h.tensor_tensor(out=rstd_all[:nt, t:t + 1], in0=msq[:nt, s_, 1:2], in1=neghalf[:nt, 0:1],
                                           op=ALU.pow), [B("msq%db" % s_), B("neghalf")], [B("rstd%d" % t)])
            dve(lambda h: h.tensor_scalar(out=hr_all[:nt, t:t + 1], in0=rstd_all[:nt, t:t + 1], scalar1=0.5,
                                          scalar2=None, op0=ALU.mult), [B("rstd%d" % t)], [B("hr%d" % t)])

        def xT(t):
            if t is None:
                return
            nt, r0 = tile_geo(t)
            s_ = S2(t)
            s3 = S3(t)
            bxb, bhT = B("xb%d" % s3), B("hT%d" % s_)
            pb4 = pbank(t)
            tp_v = tpv(pb4)
            for kc in range(8):
                tr(tp_v[:, kc, 0:nt], xb[:nt, s3, kc * 128:(kc + 1) * 128], ident_b[:nt, :nt],
                   [bxb, B("ident_b")], [BK(pb4)])
            if False:
                act(lambda h: h.copy(out=hT[:, s_, :, 0:nt], in_=tp_v[:, :, 0:nt]), [], [bhT, BK(pb4)])
            else:
                dve(lambda h: h.tensor_copy(out=hT[:, s_, :, 0:nt], in_=tp_v[:, :, 0:nt]), [], [bhT, BK(pb4)])

        def gen_P(t):
            nt, r0 = tile_geo(t)
            s_ = S2(t)
            s3 = S3(t)
            g_ = t % 4
            Gt = G if t < NPT else Gs
            bxb, bhT = B("xb%d" % s3), B("hT%d" % s_)
            on_act = (t < ACT_TILES)
            pb4 = pbank(t)
            tq_v = tqv(pb4)
            for kc in range(8):
                mm(ps[:nt, pb4, :], hT[:, s_, kc, 0:nt], Wg[:, kc, 0:512], kc == 0, [bhT, B("Wg")], [BK(pb4)])
                if kc == 3:
                    yield
            rs = rstd_all[:nt, t:t + 1]
            brs = B("rstd%d" % t)
            dve(lambda h: h.tensor_scalar(out=rb[:nt, s_, :], in0=ps[:nt, pb4, 256:512], scalar1=rs, scalar2=None,
                                          op0=ALU.mult), [brs], [B("rb%d" % s_), BK(pb4)])
            if on_act:
                act(lambda h: h.activation(out=qk_bf[:nt, s_, 0:256], in_=ps[:nt, pb4, 0:256], func=AF.Copy, scale=rs),
                    [brs], [B("qkbf%da" % s_), BK(pb4)])
            else:
                dve(lambda h: h.tensor_scalar(out=qk_bf[:nt, s_, 0:256], in0=ps[:nt, pb4, 0:256], scalar1=rs,
                                              scalar2=None, op0=ALU.mult), [brs], [B("qkbf%da" % s_), BK(pb4)])
            want_a = (t >= NPT - 4)
            if want_a:
                dve(lambda h: h.tensor_scalar(out=kastage[:nt, s_, :], in0=ps[:nt, pb4, 128:256], scalar1=rs,
                                              scalar2=None, op0=ALU.mult), [brs], [B("kast%d" % s_), BK(pb4)])
            rbv = rb[:nt, s_, :].rearrange("p (g d) -> p g d", g=4)
            x1 = rbv[:, :, 0:8]
            x2 = rbv[:, :, 8:16]
            s4 = pos[t] % 4
            csv = cs_t[:nt, s4, :].rearrange("p (a g d) -> p a g d", a=2, g=4)
            cos4 = csv[:, 0, :, :]
            sin4 = csv[:, 1, :, :]
            tmpv = rtmp[:nt, s_, :, :].rearrange("p a (g d) -> p a g d", g=4)
            brb, bcs = B("rb%d" % s_), B("cs%d" % s4)
            dve(lambda h: h.tensor_tensor(out=tmpv[:, 0], in0=x1, in1=cos4, op=ALU.mult), [brb, bcs], [B("rtA%d" % s_)])
            dve(lambda h: h.tensor_tensor(out=tmpv[:, 1], in0=x2, in1=sin4, op=ALU.mult), [brb, bcs], [B("rtB%d" % s_)])
            dve(lambda h: h.tensor_tensor(out=tmpv[:, 2], in0=x2, in1=cos4, op=ALU.mult), [brb, bcs], [B("rtC%d" % s_)])
            dve(lambda h: h.tensor_tensor(out=tmpv[:, 3], in0=x1, in1=sin4, op=ALU.mult), [brb, bcs], [B("rtD%d" % s_)])
            dve(lambda h: h.tensor_tensor(out=x1, in0=tmpv[:, 0], in1=tmpv[:, 1], op=ALU.subtract),
                [B("rtA%d" % s_), B("rtB%d" % s_), B("rtD%d" % s_)], [brb])
            dve(lambda h: h.tensor_tensor(out=x2, in0=tmpv[:, 2], in1=tmpv[:, 3], op=ALU.add),
                [B("rtC%d" % s_), B("rtD%d" % s_)], [brb])
            act(lambda h: h.copy(out=qk_bf[:nt, s_, 256:512], in_=rb[:nt, s_, :]), [brb], [B("qkbf%db" % s_)])
            yield
            for j in range(4):
                tr(tq_v[:, j, 0:nt], qk_bf[:nt, s_, j * 128:(j + 1) * 128], ident_b[:nt, :nt],
                   [B("qkbf%da" % s_), B("qkbf%db" % s_), B("ident_b")], [BK(pb4)])
            dve(lambda h: h.tensor_copy(out=QA[0:64, s_, 0, 0:nt], in_=tq_v[0:64, 0, 0:nt]), [],
                [B("QA%d" % s_), BK(pb4)])
            dve(lambda h: h.tensor_copy(out=QA[64:128, s_, 1, 0:nt], in_=tq_v[64:128, 0, 0:nt]), [],
                [B("QA%d" % s_), BK(pb4)])
            dve(lambda h: h.tensor_copy(out=K2[:, :, r0:r0 + nt], in_=tq_v[:, 1:4:2, 0:nt]),
                [], [B("K2_%d" % t), BK(pb4)])
            if t < NPT:
                qs = (t // 2) % 2
                c0 = (t % 2) * 128
                act(lambda h: h.copy(out=QB[0:64, qs, c0:c0 + 128], in_=tq_v[0:64, 2, :]), [],
                    [B("QB%d_%d" % (qs, t % 2)), BK(pb4)])
                act(lambda h: h.copy(out=QB[64:128, qs, 256 + c0:256 + c0 + 128], in_=tq_v[64:128, 2, :]), [],
                    [B("QB%d_%d" % (qs, t % 2)), BK(pb4)])
            else:
                dve(lambda h: h.tensor_copy(out=QBs[:, t - NPT, :], in_=tq_v[:, 2, 0:nt]), [],
                    [B("QBs%d" % (t - NPT)), BK(pb4)])
            yield
            for kc in range(8):
                mm(ps[:nt, pb4, :], hT[:, s_, kc, 0:nt], Wg[:, kc, 512:1024], kc == 0, [bhT, B("Wg")], [BK(pb4)])
                if kc == 3:
                    yield
            act(lambda h: h.activation(out=th[:nt, s_, :], in_=ps[:nt, pb4, 256:512], func=AF.Tanh,
                                       scale=hr_all[:nt, t:t + 1]), [B("hr%d" % t)], [B("th%d" % s_), BK(pb4)])
            if on_act:
                act(lambda h: h.activation(out=vstage[:nt, s_, :], in_=ps[:nt, pb4, 0:256], func=AF.Copy, scale=rs),
                    [brs], [B("vst%d" % s_), BK(pb4)])
            else:
                dve(lambda h: h.tensor_scalar(out=vstage[:nt, s_, :], in0=ps[:nt, pb4, 0:256], scalar1=rs, scalar2=None,
                                              op0=ALU.mult), [brs], [B("vst%d" % s_), BK(pb4)])
            dve(lambda h: h.scalar_tensor_tensor(out=Gt[:nt, g_, :], in0=th[:nt, s_, :], scalar=1.0,
                                                 in1=ps[:nt, pb4, 256:512], op0=ALU.add, op1=ALU.mult),
                [B("th%d" % s_)], [B("G%d_%d" % (t >= NPT, g_)), BK(pb4)])
            dma(kb_out.ap()[r0:r0 + nt, :], rb[:nt, s_, 128:256], [brb], [], "rb%d" % s_)
            dma(vb_out.ap()[r0:r0 + nt, :], vstage[:nt, s_, 128:256], [B("vst%d" % s_)], [], "vst%d" % s_)
            if want_a:
                if t < NPT:
                    o0 = 128 * (t - (NPT - 4))
                    dma(akp_out.ap()[o0:o0 + 128, :], kastage[:nt, s_, :], [B("kast%d" % s_)], [], "kast%d" % s_)
                    dma(avp_out.ap()[o0:o0 + 128, :], vstage[:nt, s_, 0:128], [B("vst%d" % s_)], [], "vsta%d" % s_)
                else:
                    bb = t - NPT
                    dma(aks_out.ap()[bb, 480:512, :], kastage[:nt, s_, :], [B("kast%d" % s_)], [], "kast%d" % s_)
                    dma(avs_out.ap()[bb, 480:512, :], vstage[:nt, s_, 0:128], [B("vst%d" % s_)], [], "vsta%d" % s_)
            pool(lambda h: h.tensor_copy(out=Va[:nt, va_idx(t), :, 0:64],
                                         in_=vstage[:nt, s_, 0:128].rearrange("p (h d) -> p h d", h=2)),
                 [B("vst%d" % s_), B("Va_ones")], [B("Va%d" % va_idx(t))])
            pool(lambda h: h.tensor_copy(out=Vb[:nt, t, 0:128], in_=vstage[:nt, s_, 128:256]),
                 [B("vst%d" % s_), B("Vb_ones")], [B("Vb%d" % t)])
            yield

        def epilogue_A(t, nt):
            s_ = S2(t)
            g_ = t % 4
            Gt = G if t < NPT else Gs
            OGt = OG if t < NPT else OGs
            e_ = epi[:nt, s_, :]
            ab6 = abank(t)
            accA = accAv(ab6)
            dve(lambda h: h.reciprocal(out=e_[:, 0:2], in_=accA[:nt, :, 64]), [], [B("epiA%d" % s_), BK(ab6)])
            dve(lambda h: h.tensor_scalar(out=e_[:, 2:4], in0=e_[:, 0:2], scalar1=hr_all[:nt, t:t + 1], scalar2=None,
                                          op0=ALU.mult), [B("epiA%d" % s_), B("hr%d" % t)], [B("epiA2%d" % s_)])
            for hh in range(2):
                dve(lambda h, hh=hh: h.scalar_tensor_tensor(out=OGt[:nt, g_, hh * 64:(hh + 1) * 64],
                                                            in0=accA[:nt, hh, 0:64], scalar=e_[:, 2 + hh:3 + hh],
                                                            in1=Gt[:nt, g_, hh * 64:(hh + 1) * 64],
                                                            op0=ALU.mult, op1=ALU.mult),
                    [B("epiA2%d" % s_), B("G%d_%d" % (t >= NPT, g_))],
                    [B("OGa%d_%d" % (t >= NPT, g_)), BK(ab6)])

        def gen_A(t):
            s_ = S2(t)
            ab6 = abank(t)
            accA = accAv(ab6)
            ndj = min(t, 4)
            bq = B("QA%d" % s_)
            if ndj > 0:
                for hh in range(2):
                    for dj in range(ndj, 0, -1):
                        j = t - dj
                        mm(ps[:, ab6, (dj - 1) * 128:dj * 128], K2[:, 0, j * 128:(j + 1) * 128],
                           QA[:, s_, hh, :], dj == ndj, [B("K2_%d" % j), bq], [BK(ab6)])
                    mm(ps[:, ab6, 0:ndj * 128], ident_b[:],
                       EBm[:, hh, 1:1 + ndj, :].rearrange("p r q -> p (r q)"), False,
                       [B("ident_b"), B("EBm")], [BK(ab6)])
                    act(lambda h, hh=hh: h.activation(
                        out=E_A[:, s_, hh, 1:1 + ndj, :].rearrange("p r q -> p (r q)"),
                        in_=ps[:, ab6, 0:ndj * 128], func=AF.Exp, scale=0.125),
                        [], [B("EA%d_%d" % (s_, hh)), BK(ab6)])
                    yield
            for hh in range(2):
                mm(ps[:, ab6, hh * 128:(hh + 1) * 128], K2[:, 0, t * 128:(t + 1) * 128],
                   QA[:, s_, hh, :], hh == 0, [B("K2_%d" % t), bq], [BK(ab6)])
            for hh in range(2):
                mm(ps[:, ab6, hh * 128:(hh + 1) * 128], ident_b[:], EBm[:, hh, 0, :], False,
                   [B("ident_b"), B("EBm")], [BK(ab6)])
            act(lambda h: h.activation(out=E_A[:, s_, :, 0, :],
                                       in_=ps[:, ab6, 0:256].rearrange("p (h q) -> p h q", h=2), func=AF.Exp,
                                       scale=0.125), [], [B("EAd%d_0" % s_), B("EAd%d_1" % s_), BK(ab6)])
            yield
            first = True
            for hh in range(2):
                for dj in range(ndj, -1, -1):
                    j = t - dj
                    mm(accA[:, hh, :], E_A[:, s_, hh, dj, :], Va[:, j % 8, hh, :], first,
                       [B("EA%d_%d" % (s_, hh)), B("EAd%d_%d" % (s_, hh)), B("Va%d" % (j % 8))], [BK(ab6)])
                    first = False
            epilogue_A(t, 128)
            yield

        og_defer = []
        og_round = [0]

        def og_flush(age=4):
            og_round[0] += 1
            while og_defer and (og_round[0] - og_defer[0][2] >= age):
                t_, nt_, _ = og_defer.pop(0)
                og_store(t_, nt_)

        def og_store(t, nt):
            g_ = t % 4
            OGt = OG if t < NPT else OGs
            sfx = "%d_%d" % (t >= NPT, g_)
            ab6 = abank(t)
            og_v = ogv(ab6)
            if t < NPT:
                grp = t // 4
                slot = grp % 2
                c0 = (t % 4) * 128
            else:
                grp = 16
                slot = 0
                c0 = (t - NPT) * 32
            for blk in range(2):
                tr(og_v[:, blk, 0:nt], OGt[:nt, g_, blk * 128:(blk + 1) * 128], ident_b[:nt, :nt],
                   [B("OGa" + sfx), B("OGb" + sfx), B("ident_b")], [BK(ab6)])
            dve(lambda h: h.tensor_copy(out=OGT[:, slot, :, c0:c0 + nt], in_=og_v[:, 0:2, 0:nt]),
                [], [B("OGT%d" % slot), BK(ab6)])
            last = (t % 4 == 3) if t < NPT else (t == NT - 1)
            if last:
                ch, coff = chunk_of(grp)
                ncol = 512 if t < NPT else 128
                dst = o_bounce[ch].ap().rearrange("(b p) t -> p b t", p=128)[:, :, coff:coff + ncol]
                src = OGT[:, slot, :, 0:ncol]
                dma(dst, src, [B("OGT%d" % slot)], [B("o_bounce%d" % ch)], "ogt%d" % slot)
                xs["chunk_cnt"][ch] += 1
                if xs["chunk_cnt"][ch] == (3 if ch < 5 else 2):
                    xs["ag_wait"].append((ch, xs["tile"]))

        def epilogue_B(t, nt, acc_s, sbuf_acc=False):
            s_ = t % 2
            g_ = t % 4
            Gt = G if t < NPT else Gs
            OGt = OG if t < NPT else OGs
            sfx = "%d_%d" % (t >= NPT, g_)
            a0, a1 = acc_s
            e_ = epi[:nt, 2 + s_, :]
            bacc = [B("accS0"), B("accS1")] if sbuf_acc else [BK(2), BK(3)]
            dve(lambda h: h.reciprocal(out=e_[:, 0:1], in_=a0[:nt, 128:129]), ([bacc[0]] if sbuf_acc else []), [B("eB0%d" % s_)] + ([] if sbuf_acc else [bacc[0]]))
            dve(lambda h: h.reciprocal(out=e_[:, 1:2], in_=a1[:nt, 128:129]), ([bacc[1]] if sbuf_acc else []), [B("eB1%d" % s_)] + ([] if sbuf_acc else [bacc[1]]))
            dve(lambda h: h.tensor_scalar(out=e_[:, 2:3], in0=e_[:, 1:2], scalar1=lsc[:nt, 4:5], scalar2=None,
                                          op0=ALU.mult), [B("eB1%d" % s_), B("neglam")], [B("eB2%d" % s_)])
            dve(lambda h: h.tensor_scalar(out=t1[:nt, s_, :], in0=a1[:nt, 0:128], scalar1=e_[:, 2:3], scalar2=None,
                                          op0=ALU.mult), [B("eB2%d" % s_)] + ([bacc[1]] if sbuf_acc else []), [B("t1_%d" % s_)] + ([] if sbuf_acc else [bacc[1]]))
            dve(lambda h: h.scalar_tensor_tensor(out=dd[:nt, s_, :], in0=a0[:nt, 0:128], scalar=e_[:, 0:1],
                                                 in1=t1[:nt, s_, :], op0=ALU.mult, op1=ALU.add),
                [B("eB0%d" % s_), B("t1_%d" % s_)] + ([bacc[0]] if sbuf_acc else []), [B("dd%d" % s_)] + ([] if sbuf_acc else [bacc[0]]))
            act(lambda h: h.activation(out=junk[:nt, 128:256], in_=dd[:nt, s_, :], func=AF.Square,
                                       accum_out=e_[:, 3:4]),
                [B("dd%d" % s_)], [B("eB3%d" % s_), B("junkc")])
            dve(lambda h: h.tensor_scalar(out=e_[:, 4:5], in0=e_[:, 3:4], scalar1=1.0 / 128.0, scalar2=float(EPS),
                                          op0=ALU.mult, op1=ALU.add), [B("eB3%d" % s_)], [B("eB4%d" % s_)])
            pool(lambda h: h.tensor_tensor(out=e_[:, 5:6], in0=e_[:, 4:5], in1=neghalf[:nt, 0:1], op=ALU.pow),
                 [B("eB4%d" % s_), B("neghalf")], [B("eB5%d" % s_)])
            dve(lambda h: h.tensor_scalar(out=e_[:, 6:7], in0=e_[:, 5:6], scalar1=hr_all[:nt, t:t + 1], scalar2=None,
                                          op0=ALU.mult), [B("eB5%d" % s_), B("hr%d" % t)], [B("eB6%d" % s_)])
            dve(lambda h: h.scalar_tensor_tensor(out=OGt[:nt, g_, 128:256], in0=dd[:nt, s_, :], scalar=e_[:, 6:7],
                                                 in1=Gt[:nt, g_, 128:256], op0=ALU.mult, op1=ALU.mult),
                [B("dd%d" % s_), B("eB6%d" % s_), B("G" + sfx)], [B("OGb" + sfx)])
            og_defer.append((t, nt, og_round[0]))

        def gen_B(I):
            qs = I % 2
            nsteps = 2 * I + 2
            bq = [B("QB%d_0" % qs), B("QB%d_1" % qs)]

            SB = (0, 1, 5)

            def scores(j):
                mm(ps[:, SB[j % 3], :], K2[:, 1, j * 128:(j + 1) * 128], QB[:, qs, :], True,
                   [B("K2_%d" % j)] + bq, [BK(SB[j % 3])])

            scores(0)
            if nsteps > 1:
                scores(1)
            for j in range(nsteps):
                if j + 2 < nsteps:
                    scores(j + 2)
                c0 = 0 if j <= 2 * I else 128
                pb = j % 4
                sbk = SB[j % 3]
                bP = B("PB%d" % pb)
                PBf = PB[:, pb, :, :].rearrange("p m q -> p (m q)")
                PBv = PBf[:, 0:512].rearrange("p (m q) -> p m q", m=2)
                if c0 == 0:
                    act(lambda h, sbk=sbk, PBf=PBf: h.activation(out=PBf[:, 0:512], in_=ps[:, sbk, :], func=AF.Exp,
                                                                 scale=0.125), [], [bP, BK(sbk)])
                else:
                    act(lambda h, sbk=sbk, PBv=PBv: h.activation(
                        out=PBv[:, :, 128:256], in_=ps[:, sbk, :].rearrange("p (m q) -> p m q", m=2)[:, :, 128:256],
                        func=AF.Exp, scale=0.125), [], [bP, BK(sbk)])
                if j >= 2 * I:
                    d0 = 0 if j == 2 * I else 128
                    pool(lambda h, PBv=PBv, d0=d0: h.memset(PBv[64:128, :, d0:d0 + 64], 0.0), [bP], [bP])
                for m in range(2):
                    for s in range(2):
                        if s * 128 < c0:
                            continue
                        mm(accB[m][:, s, :], PBf[:, m * 256 + s * 128:m * 256 + (s + 1) * 128], Vb[:, j, :],
                           (j == 0 and s == 0), [bP, B("Vb%d" % j)], [BK(2 + m)])
                yield

        def B_evac(I):
            for m in range(2):
                dve(lambda h, m=m: h.tensor_copy(out=accS[:, m, :, :], in_=accB[m][:, :, :]), [],
                    [B("accS%d" % m), BK(2 + m)])

        def B_epilogues(I):
            for s in range(2):
                epilogue_B(2 * I + s, 128, (accS[:, 0, s, :], accS[:, 1, s, :]), sbuf_acc=True)

        def cache_key_loads(bb):
            if bb >= 4:
                return
            sl = bb % 2
            dma(ck16a[:, sl, :, :], c_ak.ap()[bb].rearrange("(k p) c -> p k c", p=128), [], [B("ck16a%d" % sl)],
                "ck16a%d" % sl, eng="pool")
            dma(ck16b[:, sl, :, :], c_bk.ap()[bb].rearrange("(k p) c -> p k c", p=128), [], [B("ck16b%d" % sl)],
                "ck16b%d" % sl, eng="pool")

        def sample_cache(bb):
            sl = bb % 2
            pb4 = pbank(NPT + bb)
            tp_v = tpv(pb4)
            if bb == 0:
                cache_key_loads(0)
            for hh in range(2):
                dma(CVA[:, :, hh, 0:64],
                    c_av.ap()[bb].rearrange("(k p) (h d) -> p k h d", p=128, h=2)[:, :, hh, :], [B("CVA_ones")],
                    [B("CVA%d" % hh)], "cva%d" % hh, eng="pool")
            dma(CVB[:, :, 0:128], c_bv.ap()[bb].rearrange("(k p) c -> p k c", p=128), [B("CVB_ones")], [B("CVB")],
                "cvb", eng="pool")
            if DO_ROLL:
                dma(aks_out.ap()[bb, 0:480, :], c_ak.ap()[bb, 32:512, :], [], [], "roll")
                dma(avs_out.ap()[bb, 0:480, :], c_av.ap()[bb, 32:512, :], [], [], "roll")
            for k in range(4):
                tr(tp_v[:, k, :], ck16a[:, sl, k, :], ident_b[:], [B("ck16a%d" % sl), B("ident_b")], [BK(pb4)])
            dve(lambda h: h.tensor_copy(out=CKA[:].rearrange("p (k t) -> p k t", k=4), in_=tp_v[:, 0:4, :]),
                [], [B("CKA"), BK(pb4)])
            yield
            for k in range(8):
                tr(tp_v[:, k, :], ck16b[:, sl, k, :], ident_b[:], [B("ck16b%d" % sl), B("ident_b")], [BK(pb4)])
            dve(lambda h: h.tensor_copy(out=CKB[:].rearrange("p (k t) -> p k t", k=8), in_=tp_v[:, :, :]),
                [], [B("CKB"), BK(pb4)])
            cache_key_loads(bb + 1)
            yield

        def gen_SA(t):
            nt, r0 = tile_geo(t)
            s_ = S2(t)
            ab6 = abank(t)
            accA = accAv(ab6)
            bq = B("QA%d" % s_)
            bEl = [B("EA%d_0" % s_), B("EA%d_1" % s_), B("EAd%d_0" % s_), B("EAd%d_1" % s_)]
            first = True
            for hh in range(2):
                for dj in range(4, 0, -1):
                    kt = 4 - dj
                    c0 = (dj - 1) * 128 + hh * 32
                    mm(ps[:, ab6, c0:c0 + 32], CKA[:, kt * 128:(kt + 1) * 128],
                       QA[:, s_, hh, 0:32], first, [B("CKA"), bq], [BK(ab6)])
                    first = False
                mm(ps[0:32, ab6, 64 + hh * 32:96 + hh * 32], K2[:, 0, r0:r0 + 32], QA[:, s_, hh, 0:32], False,
                   [B("K2_%d" % t), bq], [BK(ab6)])
            for hh in range(2):
                for dj in range(4, 0, -1):
                    c0 = (dj - 1) * 128 + hh * 32
                    mm(ps[:, ab6, c0:c0 + 32], ident_b[:], EBu[:, hh, dj, 0:32], False,
                       [B("ident_b"), B("EBu")], [BK(ab6)])
                mm(ps[0:32, ab6, 64 + hh * 32:96 + hh * 32], ident_b[0:32, 0:32], EBu[0:32, hh, 0, 0:32], False,
                   [B("ident_b"), B("EBu")], [BK(ab6)])
            for hh in range(2):
                act(lambda h, hh=hh: h.activation(
                    out=E_A[:, s_, hh, 1:5, 0:32],
                    in_=ps[:, ab6, :].rearrange("p (r q) -> p r q", r=4)[:, :, hh * 32:hh * 32 + 32], func=AF.Exp,
                    scale=0.125), [], bEl + [BK(ab6)])
                act(lambda h, hh=hh: h.activation(out=E_A[0:32, s_, hh, 0, 0:32],
                                                  in_=ps[0:32, ab6, 64 + hh * 32:96 + hh * 32],
                                                  func=AF.Exp, scale=0.125), [], bEl + [BK(ab6)])
            yield
            first = True
            for hh in range(2):
                for dj in range(4, 0, -1):
                    kt = 4 - dj
                    mm(accA[0:32, hh, :], E_A[:, s_, hh, dj, 0:32], CVA[:, kt, hh, :], first,
                       bEl + [B("CVA0"), B("CVA1")], [BK(ab6)])
                    first = False
                mm(accA[0:32, hh, :], E_A[0:32, s_, hh, 0, 0:32], Va[0:32, va_idx(t), hh, :], False,
                   bEl + [B("Va%d" % va_idx(t))], [BK(ab6)])
            epilogue_A(t, 32)
            yield

        def sample_attn_B(t):
            bb = t - NPT
            nt, r0 = tile_geo(t)
            for m in range(2):
                ms = slice(m * 64, (m + 1) * 64)
                for k in range(8):
                    mm(ps[:, m, k * 32:(k + 1) * 32], CKB[ms, k * 128:(k + 1) * 128], QBs[ms, bb, :], True,
                       [B("CKB"), B("QBs%d" % bb)], [BK(m)])
                mm(ps[0:32, m, 256:288], K2[ms, 1, r0:r0 + 32], QBs[ms, bb, :], True,
                   [B("K2_%d" % t), B("QBs%d" % bb)], [BK(m)])
            bP = B("PB0")
            act(lambda h: h.activation(out=PB[:, 0, :, 0:256], in_=ps[:, 0:2, 0:256], func=AF.Exp, scale=0.125),
                [], [bP, BK(0), BK(1)])
            act(lambda h: h.activation(out=PB[0:32, 0, :, 256:288], in_=ps[0:32, 0:2, 256:288], func=AF.Exp,
                                       scale=0.125), [], [bP, BK(0), BK(1)])
            for m in range(2):
                for k in range(8):
                    mm(accB[m][0:32, 0, :], PB[:, 0, m, k * 32:(k + 1) * 32], CVB[:, k, :], k == 0,
                       [bP, B("CVB")], [BK(2 + m)])
                mm(accB[m][0:32, 0, :], PB[0:32, 0, m, 256:288], Vb[0:32, t, :], False,
                   [bP, B("Vb%d" % t)], [BK(2 + m)])
            epilogue_B(t, 32, (accB[0][:, 0, :], accB[1][:, 0, :]))

        xs = {"chunk_cnt": [0] * 6, "ag_wait": [], "p2_wait": [], "p2_gens": [], "tile": 0, "fin_wait": [],
              "fin_cnt": 0}

        def groups_of(ch):
            return [3 * ch, 3 * ch + 1, 3 * ch + 2] if ch < 5 else [15, 16]

        def p2_load_oT(u):
            us = u % 2
            ntok = 512 if u < 16 else 128
            ch, coff = chunk_of(u)
            o_all_v = o_all[ch].ap().rearrange("(c p) t -> p c t", p=128)
            dma(oT[:, us, :, 0:ntok], o_all_v[:, :, coff:coff + ntok], [B("o_all%d" % ch)], [B("oT%d" % us)],
                "oT%d" % us)

        def p2_load_xr(ti):
            ys = ti % 3
            dma(xr[:, ys, :], xres.ap()[ti * 128:ti * 128 + 128, :], [], [B("xr%d" % ys)], "xr%d" % ys)

        def gen_chunk(ch):
            grp = groups_of(ch)
            tis = []
            for u in grp:
                tis += [(u, sub) for sub in range(4 if u < 16 else 1)]
            p2_load_oT(grp[0])
            p2_load_xr(tis[0][0] * 4 + tis[0][1])
            yield
            yield
            for i_, (u, sub) in enumerate(tis):
                us = u % 2
                ti = u * 4 + sub
                r0 = ti * 128
                ys = ti % 3
                pb_ = 6 + (ti % 2)
                if i_ + 1 < len(tis):
                    un, subn = tis[i_ + 1]
                    p2_load_xr(un * 4 + subn)
                    if subn == 0:
                        p2_load_oT(un)
                for cc in range(8):
                    mm(ps[:, pb_, 0:256], oT[:, us, cc, sub * 128:(sub + 1) * 128], Wo[:, cc, :], cc == 0,
                       [B("oT%d" % us), B("Wo")], [BK(pb_)])
                dve(lambda h, ys=ys, pb_=pb_: h.tensor_tensor(out=yt[:, ys, :], in0=ps[:, pb_, 0:256], in1=xr[:, ys, :],
                                                              op=ALU.add),
                    [B("xr%d" % ys)], [B("yt%d" % ys), BK(pb_)])
                act(lambda h, ys=ys, ti=ti: h.activation(out=xsq[:, 0:256], in_=yt[:, ys, :], func=AF.Square,
                                                         accum_out=ssq[:, ti:ti + 1]),
                    [B("yt%d" % ys)], [B("xsq"), B("ssq")])
                dma(y_scr.ap()[r0:r0 + 128, :], yt[:, ys, :], [B("yt%d" % ys)], [B("y_scr%d" % ti)], "yt%d" % ys)
                yield
            t0_ = 12 * ch
            n_ = CH_TILES[ch]
            dma(ss_b[ch].ap(), ssq[:, t0_:t0_ + n_], [B("ssq")], [B("ss_b%d" % ch)], "ssb%d" % ch)
            P.op("pool", lambda h, ch=ch: h.collective_compute(
                "AllReduce", ALU.add, replica_groups=[[0, 1, 2, 3], [4, 5, 6, 7]],
                ins=[ss_b[ch].ap().opt()], outs=[ss_a[ch].ap().opt()]),
                 reads=[B("ss_b%d" % ch)], writes=[B("ss_a%d" % ch)], akey="ar%d" % ch, aamt=1)
            xs["fin_wait"].append((ch, xs["tile"]))
            yield

        def gen_FIN(ch):
            t0_ = 12 * ch
            n_ = CH_TILES[ch]
            brs = B("rstdf%d" % ch)
            grp = groups_of(ch)

            def geo(u):
                nsub = 4 if u < 16 else 1
                k = u % 2
                return nsub, k, fst[:, k, 0:nsub * 256].rearrange("p (s c) -> p s c", s=nsub), [B("fst%d" % k)]

            def fload(u):
                nsub, k, stg, bst = geo(u)
                r0 = u * 512
                dma(stg, y_scr.ap()[r0:r0 + nsub * 128, :].rearrange("(s p) c -> p s c", p=128),
                    [B("y_scr%d" % (u * 4 + i)) for i in range(nsub)], bst, "fstl%d" % k)

            dma(ssr[:, t0_:t0_ + n_], ss_a[ch].ap(), [B("ss_a%d" % ch)], [B("ssr%d" % ch)], "ssr%d" % ch)
            fload(grp[0])
            yield
            dve(lambda h: h.tensor_scalar(out=ssr[:, t0_:t0_ + n_], in0=ssr[:, t0_:t0_ + n_], scalar1=1.0 / 1024.0,
                                          scalar2=float(EPS), op0=ALU.mult, op1=ALU.add),
                [B("ssr%d" % ch)], [B("ssr%d" % ch)])
            pool(lambda h: h.tensor_tensor(out=rstdf[:, t0_:t0_ + n_], in0=ssr[:, t0_:t0_ + n_],
                                           in1=neghalf[:, 0:n_], op=ALU.pow), [B("ssr%d" % ch), B("neghalf")], [brs])
            yield
            for i_, u in enumerate(grp):
                nsub, k, stg, bst = geo(u)
                r0 = u * 512
                if i_ + 1 < len(grp):
                    fload(grp[i_ + 1])
                for sub in range(nsub):
                    ti = u * 4 + sub
                    dve(lambda h, stg=stg, sub=sub, ti=ti: h.scalar_tensor_tensor(
                        out=stg[:, sub, :], in0=stg[:, sub, :], scalar=rstdf[:, ti:ti + 1], in1=fgb[:],
                        op0=ALU.mult, op1=ALU.mult), [brs, B("fgb")], bst)
                dma(y_out.ap()[r0:r0 + nsub * 128, :].rearrange("(s p) c -> p s c", p=128), stg, bst, [],
                    "fsts%d" % k)
                yield

        def exchange_tick(flush=False):
            if not (DO_P2 and DO_CC):
                return
            for item in list(xs["ag_wait"]):
                ch, t0_ = item
                if flush or xs["tile"] - t0_ >= 2:
                    xs["ag_wait"].remove(item)
                    P.op("pool", lambda h, ch=ch: h.collective_compute(
                        "AllGather", ALU.bypass, replica_groups=[[0, 1, 2, 3], [4, 5, 6, 7]],
                        ins=[o_bounce[ch].ap().opt()], outs=[o_all[ch].ap().opt()]),
                         reads=[B("o_bounce%d" % ch)], writes=[B("o_all%d" % ch)], akey="ag%d" % ch, aamt=1)
                    xs["p2_wait"].append((ch, xs["tile"]))
            for item in list(xs["p2_wait"]):
                ch, t0_ = item
                if flush or xs["tile"] - t0_ >= 4:
                    xs["p2_wait"].remove(item)
                    xs["p2_gens"].append(gen_chunk(ch))
            for item in list(xs["fin_wait"]):
                ch, t0_ = item
                if flush or xs["tile"] - t0_ >= 4:
                    xs["fin_wait"].remove(item)
                    xs["p2_gens"].append(gen_FIN(ch))

        def p2_step(n):
            for _ in range(n):
                while xs["p2_gens"]:
                    g = step(xs["p2_gens"][0])
                    if g is None:
                        xs["p2_gens"].pop(0)
                        continue
                    break

        def step(g):
            if g is None:
                return None
            try:
                next(g)
                return g
            except StopIteration:
                return None

        def run_all(g):
            while g is not None:
                g = step(g)

        pool(lambda h: h.memset(QA[64:128, :, 0, :], 0.0), [], [B("QA0"), B("QA1")])
        pool(lambda h: h.memset(QA[0:64, :, 1, :], 0.0), [], [B("QA0"), B("QA1")])
        pool(lambda h: h.memset(QB[64:128, :, 0:256], 0.0), [], [B("QB0_0"), B("QB0_1"), B("QB1_0"), B("QB1_1")])
        pool(lambda h: h.memset(QB[0:64, :, 256:512], 0.0), [], [B("QB0_0"), B("QB0_1"), B("QB1_0"), B("QB1_1")])

        def interleave(gens, fill=True):
            gens = list(gens)
            while any(g is not None for g in gens):
                for i_, g_ in enumerate(gens):
                    gens[i_] = step(g_)
                if fill:
                    p2_step(2 if len(xs["p2_gens"]) > 1 else 1)
                    og_flush()

        def run_fill(g, every=5, early=None):
            n_ = 0
            while g is not None:
                g = step(g)
                n_ += 1
                if n_ == 1 and early is not None:
                    early()
                    early = None
                if n_ % every == 0:
                    p2_step(1)
                    og_flush()
            if early is not None:
                early()

        pairs = [tiles[k:k + 2] for k in range(0, len(tiles), 2)]
        for t_ in pairs[0]:
            load_x(t_)
        for t_ in pairs[0]:
            prep(t_)
            xT(t_)
        pend_epi = []
        for k, pr in enumerate(pairs):
            xs["tile"] = 2 * k
            exchange_tick()
            nxt_pr = pairs[k + 1] if k + 1 < len(pairs) else []
            for t_ in nxt_pr:
                load_x(t_)
            interleave([gen_P(t_) for t_ in pr], fill=False)
            xs["tile"] = 2 * k + 1
            exchange_tick()
            if pr[0] < NPT:
                interleave([gen_A(t_) for t_ in pr], fill=False)
                for t_ in nxt_pr:
                    prep(t_)
                    xT(t_)
                I_ = pr[0] // 2
                prev = list(pend_epi)
                pend_epi = []

                def early_fn(prev=prev):
                    for J_ in prev:
                        B_epilogues(J_)

                run_fill(gen_B(I_), early=early_fn)
                og_flush(age=-10 ** 6)
                B_evac(I_)
                pend_epi.append(I_)
            else:
                for t_ in pr:
                    run_all(sample_cache(t_ - NPT))
                    run_all(gen_SA(t_))
                    sample_attn_B(t_)
                    p2_step(1)
                    og_flush()
                for t_ in nxt_pr:
                    prep(t_)
                    xT(t_)
        for I_ in pend_epi:
            B_epilogues(I_)
        og_flush(age=-10 ** 6)
        if DO_P2:
            for _ in range(4):
                exchange_tick(flush=True)
                while xs["p2_gens"]:
                    p2_step(1)

        with nc.Block() as block:
            P.emit(nc, block, st)
    return nc


_NC_CACHE = {}


def _rope_table():
    half = 8
    inv = (np.float32(500000.0) ** (-np.arange(0, 16, 2, dtype=np.float32) / np.float32(16))).astype(np.float32)
    pos = np.concatenate([np.arange(S, dtype=np.float32)] + [1024 + np.arange(32, dtype=np.float32)] * 4)
    ang = (pos[:, None] * inv[None, :]).astype(np.float32)
    cos = np.cos(ang).astype(np.float32)
    sin = np.sin(ang).astype(np.float32)
    tab = np.zeros((NTOK, 2, 4, half), np.float32)
    tab[:, 0] = cos[:, None, :]
    tab[:, 1] = sin[:, None, :]
    return np.ascontiguousarray(tab.reshape(NTOK, 64))


def kernel(x_prompt, x_sample, cache_a_k, cache_a_v, cache_b_k, cache_b_v,
           norm_gain, w_in, w_out, rel_bias, lambda_q1, lambda_k1, lambda_q2, lambda_k2,
           subln_gain, final_gain):
    f = lambda a: np.ascontiguousarray(np.asarray(a, dtype=np.float32))
    x_prompt, x_sample = f(x_prompt), f(x_sample)
    cache_a_k, cache_a_v, cache_b_k, cache_b_v = f(cache_a_k), f(cache_a_v), f(cache_b_k), f(cache_b_v)
    norm_gain, w_in, w_out, rel_bias = f(norm_gain), f(w_in), f(w_out), f(rel_bias)
    subln_gain, final_gain = f(subln_gain), f(final_gain)
    lam4 = np.concatenate([f(lambda_q1)[0], f(lambda_k1)[0], f(lambda_q2)[0], f(lambda_k2)[0]])[None, :]

    if "nc" not in _NC_CACHE:
        _NC_CACHE["nc"] = build_nc()
    nc = _NC_CACHE["nc"]

    cst = _rope_table()
    ident = np.eye(128, dtype=np.float32)
    jmat = np.ascontiguousarray(ident[::-1])
    W = w_in[0]
    gain_t = np.ascontiguousarray(norm_gain[0].reshape(8, 128).T)
    in_maps = []
    for c in range(8):
        b, g = divmod(c, 4)
        sl = slice(128 * g, 128 * g + 128)
        cols = np.concatenate([np.arange(0, 128) + 128 * g + off for off in
                               (0, 512, 2048, 2560, 1024, 3072, 1536, 3584)])
        x_all = np.concatenate([x_prompt[b]] + [x_sample[4 * b + bb] for bb in range(4)], axis=0)
        rows = []
        for r in range(4):
            rows.append(np.arange(128) + 128 * r)
            rows.append(np.arange(128) + 512 + 128 * r)
        rows = np.concatenate(rows)
        w_out_g = w_out[0][rows][:, 256 * g:256 * g + 256].reshape(8, 128, 256)
        m = {
            "x_all": np.ascontiguousarray(x_all),
            "w_in": np.ascontiguousarray(W[:, cols].reshape(8, 128, 1024)),
            "gain_t": gain_t,
            "w_out": np.ascontiguousarray(w_out_g),
            "sgain": np.ascontiguousarray(subln_gain[0].reshape(128, 1)),
            "xres": np.ascontiguousarray(x_all[:, 256 * g:256 * g + 256]),
            "fgain": np.ascontiguousarray(final_gain[256 * g:256 * g + 256].reshape(1, 256)),
            "relb": np.ascontiguousarray(rel_bias[0, 2 * g:2 * g + 2]),
            "lam4": np.ascontiguousarray(lam4),
            "cst": cst,
            "c_ak": np.ascontiguousarray(cache_a_k[0, 4 * b:4 * b + 4, :, 2 * g:2 * g + 2, :].reshape(4, 512, 128)),
            "c_av": np.ascontiguousarray(cache_a_v[0, 4 * b:4 * b + 4, :, 2 * g:2 * g + 2, :].reshape(4, 512, 128)),
            "c_bk": np.ascontiguousarray(cache_b_k[0, 4 * b:4 * b + 4, :, g].reshape(4, 1024, 128)),
            "c_bv": np.ascontiguousarray(cache_b_v[0, 4 * b:4 * b + 4, :, g].reshape(4, 1024, 128)),
            "ident": ident,
            "jmat": jmat,
        }
        in_maps.append(m)

    res = run_bass_kernel_spmd(nc, in_maps, core_ids=list(range(8)))
    R = res.results

    y_prompt = np.empty((2, S, 1024), np.float32)
    y_sample = np.empty((8, 32, 1024), np.float32)
    akp = np.empty((1, 2, 512, 8, 64), np.float32)
    avp = np.empty((1, 2, 512, 8, 64), np.float32)
    bkp = np.empty((1, 2, S, 4, 2, 64), np.float32)
    bvp = np.empty((1, 2, S, 4, 128), np.float32)
    aks = np.empty((1, 8, 512, 8, 64), np.float32)
    avs = np.empty((1, 8, 512, 8, 64), np.float32)
    bks = np.empty((1, 8, 32, 4, 2, 64), np.float32)
    bvs = np.empty((1, 8, 32, 4, 128), np.float32)
    for c in range(8):
        b, g = divmod(c, 4)
        r = R[c]
        y = np.asarray(r["y_out"])
        y_prompt[b, :, 256 * g:256 * g + 256] = y[:S]
        y_sample[4 * b:4 * b + 4, :, 256 * g:256 * g + 256] = y[S:].reshape(4, 32, 256)
        kb = np.asarray(r["kb_out"])
        vb = np.asarray(r["vb_out"])
        bkp[0, b, :, g] = kb[:S].reshape(S, 2, 64)
        bvp[0, b, :, g] = vb[:S]
        bks[0, 4 * b:4 * b + 4, :, g] = kb[S:].reshape(4, 32, 2, 64)
        bvs[0, 4 * b:4 * b + 4, :, g] = vb[S:].reshape(4, 32, 128)
        akp[0, b, :, 2 * g:2 * g + 2] = np.asarray(r["akp_out"]).reshape(512, 2, 64)
        avp[0, b, :, 2 * g:2 * g + 2] = np.asarray(r["avp_out"]).reshape(512, 2, 64)
        aks[0, 4 * b:4 * b + 4, :, 2 * g:2 * g + 2] = np.asarray(r["aks_out"]).reshape(4, 512, 2, 64)
        avs[0, 4 * b:4 * b + 4, :, 2 * g:2 * g + 2] = np.asarray(r["avs_out"]).reshape(4, 512, 2, 64)
    return (y_prompt, y_sample, akp, avp, bkp, bvp, aks, avs, bks, bvs)
```

```python
import math
import numpy as np
import ml_dtypes
from contextlib import ExitStack

import concourse.bass as bass
import concourse.mybir as mybir
from concourse.bass_utils import run_bass_kernel_spmd

F32 = mybir.dt.float32
BF16 = mybir.dt.bfloat16
AF = mybir.ActivationFunctionType
ALU = mybir.AluOpType

S = 8192
NPT = 64
NT = 68
NTOK = 8320
EPS = 1e-6
LAMBDA_INIT = 0.8 - 0.6 * math.exp(0.0)
NEG = 0.0


class Buf:
    __slots__ = ("name", "last_w", "readers", "bank")

    def __init__(self, name, bank=False):
        self.name = name
        self.last_w = None
        self.readers = []
        self.bank = bank


class Op:
    __slots__ = ("eng", "fn", "deps", "needed", "inc_idx", "akey", "aval", "aamt")

    def __init__(self, eng, fn):
        self.eng = eng
        self.fn = fn
        self.deps = []
        self.needed = False
        self.inc_idx = None
        self.akey = None
        self.aval = None
        self.aamt = 16


class Prog:
    ENGS = ("pe", "act", "dve", "pool", "sp")

    def __init__(self):
        self.ops = []
        self.acount = {}
        self.group_keys = set()

    def op(self, eng, fn, reads=(), writes=(), akey=None, aamt=16, group=False):
        o = Op(eng, fn)
        deps = []
        for b in reads:
            if b.last_w is not None:
                deps.append(b.last_w)
        for b in writes:
            if b.bank:
                deps.extend(r for r in b.readers if r.eng != eng)
                continue
            if b.last_w is not None:
                deps.append(b.last_w)
            deps.extend(b.readers)
        seen = set()
        for d in deps:
            if id(d) in seen or d is o:
                continue
            seen.add(id(d))
            if d.akey is None and d.eng == eng and eng == "pe":
                continue
            o.deps.append(d)
            d.needed = True
        for b in reads:
            b.readers.append(o)
        for b in writes:
            if b.bank:
                b.readers = [r for r in b.readers if r.eng != eng] + [o]
                continue
            b.last_w = o
            b.readers = []
        if akey is not None:
            o.akey = akey
            o.aamt = aamt
            self.acount[akey] = self.acount.get(akey, 0) + aamt
            o.aval = self.acount[akey]
            if group:
                self.group_keys.add(akey)
        self.ops.append(o)
        return o

    def emit(self, nc, block, stack):
        cnt = {e: 0 for e in self.ENGS}
        for o in self.ops:
            if o.akey is None and o.needed:
                cnt[o.eng] += 1
                o.inc_idx = cnt[o.eng]
        sems = {e: stack.enter_context(nc.semaphore("s_" + e)) for e in self.ENGS}
        asems = {k: stack.enter_context(nc.semaphore("a_" + k)) for k in self.acount}
        ops = self.ops
        acount = self.acount
        group_keys = self.group_keys

        def run(engname, h):
            waited = {}
            for o in ops:
                if o.eng != engname:
                    continue
                for d in o.deps:
                    if d.akey is not None:
                        sem = asems[d.akey]
                        val = acount[d.akey] if d.akey in group_keys else d.aval
                        key = "a_" + d.akey
                    else:
                        sem = sems[d.eng]
                        val = d.inc_idx
                        key = d.eng
                    if waited.get(key, 0) >= val:
                        continue
                    waited[key] = val
                    h.wait_ge(sem, val)
                ins = o.fn(h)
                if o.akey is not None:
                    ins.then_inc(asems[o.akey], o.aamt)
                elif o.needed:
                    ins.then_inc(sems[o.eng], 1)
            if engname == "sp":
                for k, tot in acount.items():
                    h.wait_ge(asems[k], tot)

        @block.tensor
        def _(h):
            run("pe", h)

        @block.scalar
        def _(h):
            run("act", h)

        @block.vector
        def _(h):
            run("dve", h)

        @block.gpsimd
        def _(h):
            run("pool", h)

        @block.sync
        def _(h):
            run("sp", h)


def build_nc(cfg=None):
    cfg = cfg or {}
    NTR = cfg.get('ntiles', NT)
    DO_CC = cfg.get('cc', True)
    DO_P2 = cfg.get('p2', True)
    DO_ROLL = cfg.get('roll', True)
    nc = bass.Bass("TRN2", target_bir_lowering=False)
    P = Prog()

    def din(name, shape, dt=F32):
        return nc.dram_tensor(name, shape, dt, kind="ExternalInput")

    def dout(name, shape, dt=F32):
        return nc.dram_tensor(name, shape, dt, kind="ExternalOutput")

    x_all = din("x_all", [NTOK, 1024])
    w_in = din("w_in", [8, 128, 1024])
    gain_t = din("gain_t", [128, 8])
    w_out = din("w_out", [8, 128, 256])
    sgain = din("sgain", [128, 1])
    xres = din("xres", [NTOK, 256])
    fgain = din("fgain", [1, 256])
    relb = din("relb", [2, 257])
    lam4 = din("lam4", [1, 256])
    cst = din("cst", [NTOK, 64])
    c_ak = din("c_ak", [4, 512, 128])
    c_av = din("c_av", [4, 512, 128])
    c_bk = din("c_bk", [4, 1024, 128])
    c_bv = din("c_bv", [4, 1024, 128])
    ident_d = din("ident", [128, 128])
    jmat_d = din("jmat", [128, 128])

    y_out = dout("y_out", [NTOK, 256])
    kb_out = dout("kb_out", [NTOK, 128])
    vb_out = dout("vb_out", [NTOK, 128])
    akp_out = dout("akp_out", [512, 128])
    avp_out = dout("avp_out", [512, 128])
    aks_out = dout("aks_out", [4, 512, 128])
    avs_out = dout("avs_out", [4, 512, 128])

    ext_d = nc.dram_tensor("ext_d", [2, 768], F32)
    CH_TOK = [1536] * 5 + [640]
    o_bounce = [nc.dram_tensor("o_bounce%d" % c, [256, CH_TOK[c]], BF16) for c in range(6)]
    o_all = [nc.dram_tensor("o_all%d" % c, [1024, CH_TOK[c]], BF16) for c in range(6)]

    def chunk_of(grp):
        if grp < 15:
            return grp // 3, (grp % 3) * 512
        return 5, (grp - 15) * 512
    CH_TILES = [12] * 5 + [5]
    ss_b = [nc.dram_tensor("ss_b%d" % c, [128, CH_TILES[c]], F32) for c in range(6)]
    ss_a = [nc.dram_tensor("ss_a%d" % c, [128, CH_TILES[c]], F32) for c in range(6)]
    y_scr = nc.dram_tensor("y_scr", [NTOK, 256], F32)

    st = ExitStack()
    with st:
        def sb(name, shape, dt=F32):
            return st.enter_context(nc.sbuf_tensor("sb_" + name, shape, dt))

        ps = st.enter_context(nc.psum_tensor("ps", [128, 8, 512], F32))

        ident_f = sb("ident_f", [128, 128])
        ident_b = sb("ident_b", [128, 128], BF16)
        jmat = sb("jmat", [128, 128])
        jmat_b = sb("jmat_b", [128, 128], BF16)
        Wg = sb("Wg", [128, 8, 1024], BF16)
        Wo = sb("Wo", [128, 8, 256], BF16)
        gt = sb("gt", [128, 8])
        sg = sb("sg", [128, 1])
        sg8 = sb("sg8", [128, 1])
        fgb = sb("fgb", [128, 256])
        lamv = sb("lamv", [128, 256])
        lsc = sb("lsc", [128, 8])
        neghalf = sb("neghalf", [128, 68])
        rb2 = sb("rb2", [2, 257])
        Hst = sb("Hst", [128, 2, 5, 128])
        EBf = sb("EBf", [128, 2, 5, 128])
        EBu = sb("EBu", [128, 2, 5, 128], BF16)
        EBm = sb("EBm", [128, 2, 5, 128], BF16)

        xt = sb("xt", [128, 3, 1024])
        cs_t = sb("cs_t", [128, 4, 64])
        xb = sb("xb", [128, 3, 1024], BF16)
        ssx = sb("ssx", [128, NT])
        xsq = sb("xsq", [128, 1024], BF16)
        hT = sb("hT", [128, 2, 8, 128], BF16)
        stt = sb("stt", [128, 2, 2, 6])
        mv = sb("mv", [128, 2, 2])
        msq = sb("msq", [128, 2, 2])
        rstd_all = sb("rstd_all", [128, NT])
        hr_all = sb("hr_all", [128, NT])
        qk_bf = sb("qk_bf", [128, 2, 512], BF16)
        rb = sb("rb", [128, 2, 256])
        rtmp = sb("rtmp", [128, 2, 4, 32])
        vstage = sb("vstage", [128, 2, 256])
        kastage = sb("kastage", [128, 2, 128])
        th = sb("th", [128, 2, 256])
        G = sb("G", [128, 4, 256])
        Gs = sb("Gs", [128, 4, 256])
        K2 = sb("K2", [128, 2, NTOK], BF16)
        QA = sb("QA", [128, 2, 2, 128], BF16)
        QB = sb("QB", [128, 2, 512], BF16)
        QBs = sb("QBs", [128, 4, 32], BF16)
        Va = sb("Va", [128, 12, 2, 65], BF16)
        Vb = sb("Vb", [128, NT, 129], BF16)
        E_A = sb("E_A", [128, 2, 2, 5, 128], BF16)
        PB = sb("PB", [128, 4, 2, 288], BF16)
        OG = sb("OG", [128, 4, 256], BF16)
        OGs = sb("OGs", [128, 4, 256], BF16)
        OGT = sb("OGT", [128, 2, 2, 512], BF16)
        epi = sb("epi", [128, 4, 8])
        t1 = sb("t1", [128, 2, 128])
        dd = sb("dd", [128, 2, 128])
        junk = sb("junk", [128, 256])
        ck16a = sb("ck16a", [128, 2, 4, 128], BF16)
        ck16b = sb("ck16b", [128, 2, 8, 128], BF16)
        CKA = sb("CKA", [128, 512], BF16)
        CVA = sb("CVA", [128, 4, 2, 65], BF16)
        CKB = sb("CKB", [128, 1024], BF16)
        CVB = sb("CVB", [128, 8, 129], BF16)
        oT = sb("oT", [128, 2, 8, 512], BF16)
        xr = sb("xr", [128, 3, 256])
        yt = sb("yt", [128, 3, 256])
        ssq = sb("ssq", [128, 65])
        accS = sb("accS", [128, 2, 2, 129])
        ext_sb = xr[0:2, :, :].rearrange("p a c -> p (a c)")
        zer = yt[0:2, :, :].rearrange("p a c -> p (a c)")[:, 0:512]
        fst = sb("fst", [128, 2, 1024])
        ssr = sb("ssr", [128, 65])
        rstdf = sb("rstdf", [128, 65])

        bufs = {}

        def B(name):
            b = bufs.get(name)
            if b is None:
                b = Buf(name)
                bufs[name] = b
            return b

        banks = [Buf("bank%d" % i, bank=True) for i in range(8)]

        def BK(i):
            return banks[i]

        def dma(out, in_, reads, writes, key, eng="sp", group=False):
            return P.op(eng, lambda h, out=out, in_=in_: h.dma_start(out=out, in_=in_),
                        reads=reads, writes=writes, akey=key, group=group)

        def pe(fn, reads, writes):
            return P.op("pe", fn, reads, writes)

        def act(fn, reads, writes):
            return P.op("act", fn, reads, writes)

        def dve(fn, reads, writes):
            return P.op("dve", fn, reads, writes)

        def pool(fn, reads, writes):
            return P.op("pool", fn, reads, writes)

        def mm(out, lhsT, rhs, start, reads, writes):
            return pe(lambda h: h.matmul(out, lhsT=lhsT, rhs=rhs, start=start, stop=True,
                                         skip_group_check=True), reads, writes)

        def tr(out, in_, idn, reads, writes):
            return pe(lambda h: h.transpose(out, in_, idn), reads, writes)

        def bank_bf(b, lo=0, hi=512):
            return ps[:, b, lo:hi].bitcast(BF16)

        tp_v = bank_bf(4).rearrange("p (k t) -> p k t", k=8)
        tq_v = bank_bf(4, 0, 256).rearrange("p (k t) -> p k t", k=4)
        og_v = bank_bf(7, 0, 128).rearrange("p (k t) -> p k t", k=2)
        accA = ps[:, 6, 256:386].rearrange("p (h d) -> p h d", h=2)
        accB = [ps[:, 2, 0:258].rearrange("p (s d) -> p s d", s=2),
                ps[:, 3, 0:258].rearrange("p (s d) -> p s d", s=2)]

        SET = "setup"
        dma(ident_f[:], ident_d.ap(), [], [B("ident_f")], SET, group=True)
        dma(jmat[:], jmat_d.ap(), [], [B("jmat")], SET, group=True)
        dma(gt[:], gain_t.ap(), [], [B("gt")], SET, group=True)
        dma(sg[:], sgain.ap(), [], [B("sg")], SET, group=True)
        dma(fgb[:], fgain.ap().partition_broadcast(128).rearrange("p a c -> p (a c)") if False else
            bass.AP(tensor=fgain, offset=0, ap=[[0, 128], [1, 256]]), [], [B("fgb")], SET, group=True)
        dma(lamv[:], bass.AP(tensor=lam4, offset=0, ap=[[0, 128], [1, 256]]), [], [B("lamv")], SET, group=True)
        dma(rb2[:], relb.ap(), [], [B("rb2")], SET, group=True)
        dve(lambda h: h.tensor_copy(out=ident_b[:], in_=ident_f[:]), [B("ident_f")], [B("ident_b")])
        pool(lambda h: h.memset(neghalf[:], -0.5), [], [B("neghalf")])
        pool(lambda h: h.memset(zer[:], 0.0), [], [B("zer")])
        pool(lambda h: h.memset(Vb[:, :, 128:129], 1.0), [], [B("Vb_ones")])
        pool(lambda h: h.memset(Va[:, :, :, 64:65], 1.0), [], [B("Va_ones")])
        pool(lambda h: h.memset(CVA[:, :, :, 64:65], 1.0), [], [B("CVA_ones")])
        pool(lambda h: h.memset(CVB[:, :, 128:129], 1.0), [], [B("CVB_ones")])
        dve(lambda h: h.tensor_scalar(out=sg8[:], in0=sg[:], scalar1=float(1.0 - LAMBDA_INIT), scalar2=None,
                                      op0=ALU.mult), [B("sg")], [B("sg8")])

        STG = cfg.get('stage', 99)
        for kc in range(8 if STG >= 1 else 0):
            s_ = kc % 2
            dma(xt[:, s_, :], w_in.ap()[kc], [], [B("xt%d" % s_)], "xt%d" % s_)
            dve(lambda h, kc=kc, s_=s_: h.tensor_scalar(out=Wg[:, kc, :], in0=xt[:, s_, :],
                                                        scalar1=gt[:, kc:kc + 1], scalar2=None, op0=ALU.mult),
                [B("xt%d" % s_), B("gt")], [B("Wg")])
        for cc in range(8 if STG >= 2 else 0):
            s_ = cc % 2
            dma(xt[:, s_, 0:256], w_out.ap()[cc], [], [B("xt%d" % s_)], "xt%d" % s_)
            if cc % 2 == 1:
                dve(lambda h, cc=cc, s_=s_: h.tensor_scalar(out=Wo[:, cc, :], in0=xt[:, s_, 0:256],
                                                            scalar1=sg8[:, 0:1], scalar2=None, op0=ALU.mult),
                    [B("xt%d" % s_), B("sg8")], [B("Wo")])
            else:
                dve(lambda h, cc=cc, s_=s_: h.tensor_copy(out=Wo[:, cc, :], in_=xt[:, s_, 0:256]),
                    [B("xt%d" % s_)], [B("Wo")])

        if STG >= 3:
            dve(lambda h: h.tensor_tensor(out=junk[:, 0:64], in0=lamv[:, 0:64], in1=lamv[:, 64:128], op=ALU.mult),
                [B("lamv")], [B("junk")])
            dve(lambda h: h.tensor_reduce(out=lsc[:, 0:1], in_=junk[:, 0:64], axis=mybir.AxisListType.X, op=ALU.add),
                [B("junk")], [B("lsc0")])
            dve(lambda h: h.tensor_tensor(out=junk[:, 64:128], in0=lamv[:, 128:192], in1=lamv[:, 192:256], op=ALU.mult),
                [B("lamv")], [B("junkb")])
            dve(lambda h: h.tensor_reduce(out=lsc[:, 1:2], in_=junk[:, 64:128], axis=mybir.AxisListType.X, op=ALU.add),
                [B("junkb")], [B("lsc1")])
            act(lambda h: h.activation(out=lsc[:, 2:4], in_=lsc[:, 0:2], func=AF.Exp), [B("lsc0"), B("lsc1")], [B("lsc2")])
            dve(lambda h: h.scalar_tensor_tensor(out=lsc[:, 4:5], in0=lsc[:, 3:4], scalar=float(-LAMBDA_INIT),
                                                 in1=lsc[:, 2:3], op0=ALU.add, op1=ALU.subtract),
                [B("lsc2")], [B("neglam")])

        if STG >= 4:
            dve(lambda h: h.tensor_copy(out=ext_sb[:, 0:256], in_=rb2[:, 1:257]), [B("rb2")], [B("ext_sb")])
            dve(lambda h: h.tensor_scalar(out=ext_sb[:, 256:768], in0=zer[:, 0:512], scalar1=rb2[:, 256:257],
                                          scalar2=None, op0=ALU.add), [B("rb2"), B("zer")], [B("ext_sb")])
            dma(ext_d.ap(), ext_sb[:], [B("ext_sb")], [B("ext_d")], "ext")
            dma(Hst[:], bass.AP(tensor=ext_d, offset=0, ap=[[1, 128], [768, 2], [128, 5], [1, 128]]),
                [B("ext_d")], [B("Hst")], "hst")
            Hflat = Hst[:].rearrange("p h r q -> p (h r q)")
            EBflat = EBf[:].rearrange("p h r q -> p (h r q)")
            Hhi = OGT[:].rearrange("p a b t -> p (a b t)")[:, 0:1280]
            Hlo = E_A[:].rearrange("p a h r q -> p (a h r q)")[:, 0:1280]
            if STG >= 5:
                dve(lambda h: h.tensor_copy(out=jmat_b[:], in_=jmat[:]), [B("jmat")], [B("jmat_b")])
                dve(lambda h: h.tensor_copy(out=Hhi, in_=Hflat), [B("Hst")], [B("Hhi")])
                dve(lambda h: h.tensor_tensor(out=EBflat, in0=Hflat, in1=Hhi, op=ALU.subtract),
                    [B("Hst"), B("Hhi")], [B("EBf")])
                dve(lambda h: h.tensor_copy(out=Hlo, in_=EBflat), [B("EBf")], [B("Hlo")])
                for i, (c0, c1) in enumerate(((0, 512), (512, 1024), (1024, 1280))):
                    mm(ps[:, 6, 0:c1 - c0], jmat_b[:], Hhi[:, c0:c1], True, [B("jmat_b"), B("Hhi")], [BK(6)])
                    mm(ps[:, 6, 0:c1 - c0], jmat_b[:], Hlo[:, c0:c1], False, [B("jmat_b"), B("Hlo")], [BK(6)])
                    dve(lambda h, c0=c0, c1=c1: h.tensor_copy(out=EBflat[:, c0:c1], in_=ps[:, 6, 0:c1 - c0]),
                        [], [B("EBf"), BK(6)])
            if STG >= 6:
                dve(lambda h: h.tensor_scalar(out=EBu[:], in0=EBf[:], scalar1=8.0, scalar2=None, op0=ALU.mult),
                    [B("EBf")], [B("EBu")])
                dve(lambda h: h.tensor_copy(out=EBm[:], in_=EBu[:]), [B("EBu")], [B("EBm")])
                pool(lambda h: h.memset(EBm[64:128, :, 0, 0:64], -30000.0), [B("EBm")], [B("EBm")])
                pool(lambda h: h.memset(EBm[0:64, :, 4, 64:128], -30000.0), [B("EBm")], [B("EBm")])

        if NTR >= NT:
            tiles = list(range(NPT, NT)) + list(range(NPT))
        else:
            tiles = list(range(NTR))
        pos = {t: i for i, t in enumerate(tiles)}

        def seq_next(t, k):
            i = pos[t] + k
            return tiles[i] if i < len(tiles) else None

        def S2(t):
            return pos[t] % 2

        def S3(t):
            return pos[t] % 3

        def pbank(t):
            return 4 + pos[t] % 2

        def abank(t):
            return 6 + pos[t] % 2

        def tpv(b):
            return bank_bf(b).rearrange("p (k t) -> p k t", k=8)

        def tqv(b):
            return bank_bf(b, 0, 256).rearrange("p (k t) -> p k t", k=4)

        def ogv(b):
            return bank_bf(b, 0, 128).rearrange("p (k t) -> p k t", k=2)

        def accAv(b):
            return ps[:, b, 256:386].rearrange("p (h d) -> p h d", h=2)

        def va_idx(t):
            return (t % 8) if t < NPT else 8 + (t - NPT)

        def tile_geo(t):
            if t < NPT:
                return 128, 128 * t
            return 32, S + 32 * (t - NPT)

        def load_x(t):
            if t is None:
                return
            nt, r0 = tile_geo(t)
            s3 = S3(t)
            dma(xt[:nt, s3, :], x_all.ap()[r0:r0 + nt, :], [], [B("xt%d" % s3)], "xt%d" % s3)
            s4 = pos[t] % 4
            dma(cs_t[:nt, s4, :], cst.ap()[r0:r0 + nt, :], [], [B("cs%d" % s4)], "cs%d" % s4)
            dma(xb[:nt, s3, :], x_all.ap()[r0:r0 + nt, :], [], [B("xb%d" % s3)], "xb%d" % s3, eng="pool")

        ACT_TILES = cfg.get('act_tiles', 68)

        def prep(t):
            if t is None:
                return
            nt, r0 = tile_geo(t)
            s3 = S3(t)
            s_ = S2(t)
            bx = B("xt%d" % s3)
            if t < ACT_TILES or t >= NPT:
                act(lambda h: h.activation(out=junk[:nt, 0:1024] if False else xsq[:nt, :], in_=xt[:nt, s3, :],
                                           func=AF.Square, accum_out=ssx[:nt, t:t + 1]), [bx], [B("xsq"), B("ssx%d" % t)])
                dve(lambda h: h.tensor_scalar(out=msq[:nt, s_, 1:2], in0=ssx[:nt, t:t + 1], scalar1=1.0 / 1024.0,
                                              scalar2=float(EPS), op0=ALU.mult, op1=ALU.add),
                    [B("ssx%d" % t)], [B("msq%db" % s_)])
            else:
                dve(lambda h: h.bn_stats(out=stt[:nt, s_, 0, :], in_=xt[:nt, s3, 0:512]), [bx], [B("stt%d" % s_)])
                dve(lambda h: h.bn_stats(out=stt[:nt, s_, 1, :], in_=xt[:nt, s3, 512:1024]), [bx], [B("stt%d" % s_)])
                dve(lambda h: h.bn_aggr(out=mv[:nt, s_, :], in_=stt[:nt, s_, :, :].rearrange("p a b -> p (a b)")),
                    [B("stt%d" % s_)], [B("mv%d" % s_)])
                dve(lambda h: h.scalar_tensor_tensor(out=msq[:nt, s_, 0:1], in0=mv[:nt, s_, 0:1],
                                                     scalar=mv[:nt, s_, 0:1], in1=mv[:nt, s_, 1:2], op0=ALU.mult,
                                                     op1=ALU.add), [B("mv%d" % s_)], [B("msq%da" % s_)])
                dve(lambda h: h.tensor_scalar(out=msq[:nt, s_, 1:2], in0=msq[:nt, s_, 0:1], scalar1=float(EPS),
                                              scalar2=None, op0=ALU.add), [B("msq%da" % s_)], [B("msq%db" % s_)])
            pool(lambda h: h.tensor_tensor(out=rstd_all[:nt, t:t + 1], in0=msq[:nt, s_, 1:2], in1=neghalf[:nt, 0:1],
                                           op=ALU.pow), [B("msq%db" % s_), B("neghalf")], [B("rstd%d" % t)])
            dve(lambda h: h.tensor_scalar(out=hr_all[:nt, t:t + 1], in0=rstd_all[:nt, t:t + 1], scalar1=0.5,
                                          scalar2=None, op0=ALU.mult), [B("rstd%d" % t)], [B("hr%d" % t)])

        def xT(t):
            if t is None:
                return
            nt, r0 = tile_geo(t)
            s_ = S2(t)
            s3 = S3(t)
            bxb, bhT = B("xb%d" % s3), B("hT%d" % s_)
            pb4 = pbank(t)
            tp_v = tpv(pb4)
            for kc in range(8):
                tr(tp_v[:, kc, 0:nt], xb[:nt, s3, kc * 128:(kc + 1) * 128], ident_b[:nt, :nt],
                   [bxb, B("ident_b")], [BK(pb4)])
            if t < ACT_TILES:
                act(lambda h: h.copy(out=hT[:, s_, :, 0:nt], in_=tp_v[:, :, 0:nt]), [], [bhT, BK(pb4)])
            else:
                dve(lambda h: h.tensor_copy(out=hT[:, s_, :, 0:nt], in_=tp_v[:, :, 0:nt]), [], [bhT, BK(pb4)])

        def gen_P(t):
            nt, r0 = tile_geo(t)
            s_ = S2(t)
            s3 = S3(t)
            g_ = t % 4
            Gt = G if t < NPT else Gs
            bxb, bhT = B("xb%d" % s3), B("hT%d" % s_)
            on_act = (t < ACT_TILES)
            pb4 = pbank(t)
            tq_v = tqv(pb4)
            for kc in range(8):
                mm(ps[:nt, pb4, :], hT[:, s_, kc, 0:nt], Wg[:, kc, 0:512], kc == 0, [bhT, B("Wg")], [BK(pb4)])
            rs = rstd_all[:nt, t:t + 1]
            brs = B("rstd%d" % t)
            dve(lambda h: h.tensor_scalar(out=rb[:nt, s_, :], in0=ps[:nt, pb4, 256:512], scalar1=rs, scalar2=None,
                                          op0=ALU.mult), [brs], [B("rb%d" % s_), BK(pb4)])
            if on_act:
                act(lambda h: h.activation(out=qk_bf[:nt, s_, 0:256], in_=ps[:nt, pb4, 0:256], func=AF.Copy, scale=rs),
                    [brs], [B("qkbf%da" % s_), BK(pb4)])
            else:
                dve(lambda h: h.tensor_scalar(out=qk_bf[:nt, s_, 0:256], in0=ps[:nt, pb4, 0:256], scalar1=rs,
                                              scalar2=None, op0=ALU.mult), [brs], [B("qkbf%da" % s_), BK(pb4)])
            want_a = (t >= NPT - 4)
            if want_a:
                dve(lambda h: h.tensor_scalar(out=kastage[:nt, s_, :], in0=ps[:nt, pb4, 128:256], scalar1=rs,
                                              scalar2=None, op0=ALU.mult), [brs], [B("kast%d" % s_), BK(pb4)])
            rbv = rb[:nt, s_, :].rearrange("p (g d) -> p g d", g=4)
            x1 = rbv[:, :, 0:8]
            x2 = rbv[:, :, 8:16]
            s4 = pos[t] % 4
            csv = cs_t[:nt, s4, :].rearrange("p (a g d) -> p a g d", a=2, g=4)
            cos4 = csv[:, 0, :, :]
            sin4 = csv[:, 1, :, :]
            tmpv = rtmp[:nt, s_, :, :].rearrange("p a (g d) -> p a g d", g=4)
            brb, bcs = B("rb%d" % s_), B("cs%d" % s4)
            pool(lambda h: h.tensor_tensor(out=tmpv[:, 0], in0=x1, in1=cos4, op=ALU.mult), [brb, bcs], [B("rtA%d" % s_)])
            pool(lambda h: h.tensor_tensor(out=tmpv[:, 1], in0=x2, in1=sin4, op=ALU.mult), [brb, bcs], [B("rtB%d" % s_)])
            pool(lambda h: h.tensor_tensor(out=tmpv[:, 2], in0=x2, in1=cos4, op=ALU.mult), [brb, bcs], [B("rtC%d" % s_)])
            pool(lambda h: h.tensor_tensor(out=tmpv[:, 3], in0=x1, in1=sin4, op=ALU.mult), [brb, bcs], [B("rtD%d" % s_)])
            pool(lambda h: h.tensor_tensor(out=x1, in0=tmpv[:, 0], in1=tmpv[:, 1], op=ALU.subtract),
                [B("rtA%d" % s_), B("rtB%d" % s_), B("rtD%d" % s_)], [brb])
            pool(lambda h: h.tensor_tensor(out=x2, in0=tmpv[:, 2], in1=tmpv[:, 3], op=ALU.add),
                [B("rtC%d" % s_), B("rtD%d" % s_)], [brb])
            act(lambda h: h.copy(out=qk_bf[:nt, s_, 256:512], in_=rb[:nt, s_, :]), [brb], [B("qkbf%db" % s_)])
            yield
            for j in range(4):
                tr(tq_v[:, j, 0:nt], qk_bf[:nt, s_, j * 128:(j + 1) * 128], ident_b[:nt, :nt],
                   [B("qkbf%da" % s_), B("qkbf%db" % s_), B("ident_b")], [BK(pb4)])
            dve(lambda h: h.tensor_copy(out=QA[0:64, s_, 0, 0:nt], in_=tq_v[0:64, 0, 0:nt]), [],
                [B("QA%d" % s_), BK(pb4)])
            dve(lambda h: h.tensor_copy(out=QA[64:128, s_, 1, 0:nt], in_=tq_v[64:128, 0, 0:nt]), [],
                [B("QA%d" % s_), BK(pb4)])
            dve(lambda h: h.tensor_copy(out=K2[:, :, r0:r0 + nt], in_=tq_v[:, 1:4:2, 0:nt]),
                [], [B("K2_%d" % t), BK(pb4)])
            if t < NPT:
                qs = (t // 2) % 2
                c0 = (t % 2) * 128
                act(lambda h: h.copy(out=QB[0:64, qs, c0:c0 + 128], in_=tq_v[0:64, 2, :]), [],
                    [B("QB%d_%d" % (qs, t % 2)), BK(pb4)])
                act(lambda h: h.copy(out=QB[64:128, qs, 256 + c0:256 + c0 + 128], in_=tq_v[64:128, 2, :]), [],
                    [B("QB%d_%d" % (qs, t % 2)), BK(pb4)])
            else:
                dve(lambda h: h.tensor_copy(out=QBs[:, t - NPT, :], in_=tq_v[:, 2, 0:nt]), [],
                    [B("QBs%d" % (t - NPT)), BK(pb4)])
            yield
            for kc in range(8):
                mm(ps[:nt, pb4, :], hT[:, s_, kc, 0:nt], Wg[:, kc, 512:1024], kc == 0, [bhT, B("Wg")], [BK(pb4)])
            act(lambda h: h.activation(out=th[:nt, s_, :], in_=ps[:nt, pb4, 256:512], func=AF.Tanh,
                                       scale=hr_all[:nt, t:t + 1]), [B("hr%d" % t)], [B("th%d" % s_), BK(pb4)])
            if on_act:
                act(lambda h: h.activation(out=vstage[:nt, s_, :], in_=ps[:nt, pb4, 0:256], func=AF.Copy, scale=rs),
                    [brs], [B("vst%d" % s_), BK(pb4)])
            else:
                dve(lambda h: h.tensor_scalar(out=vstage[:nt, s_, :], in0=ps[:nt, pb4, 0:256], scalar1=rs, scalar2=None,
                                              op0=ALU.mult), [brs], [B("vst%d" % s_), BK(pb4)])
            dve(lambda h: h.scalar_tensor_tensor(out=Gt[:nt, g_, :], in0=th[:nt, s_, :], scalar=1.0,
                                                 in1=ps[:nt, pb4, 256:512], op0=ALU.add, op1=ALU.mult),
                [B("th%d" % s_)], [B("G%d_%d" % (t >= NPT, g_)), BK(pb4)])
            dma(kb_out.ap()[r0:r0 + nt, :], rb[:nt, s_, 128:256], [brb], [], "rb%d" % s_)
            dma(vb_out.ap()[r0:r0 + nt, :], vstage[:nt, s_, 128:256], [B("vst%d" % s_)], [], "vst%d" % s_)
            if want_a:
                if t < NPT:
                    o0 = 128 * (t - (NPT - 4))
                    dma(akp_out.ap()[o0:o0 + 128, :], kastage[:nt, s_, :], [B("kast%d" % s_)], [], "kast%d" % s_)
                    dma(avp_out.ap()[o0:o0 + 128, :], vstage[:nt, s_, 0:128], [B("vst%d" % s_)], [], "vsta%d" % s_)
                else:
                    bb = t - NPT
                    dma(aks_out.ap()[bb, 480:512, :], kastage[:nt, s_, :], [B("kast%d" % s_)], [], "kast%d" % s_)
                    dma(avs_out.ap()[bb, 480:512, :], vstage[:nt, s_, 0:128], [B("vst%d" % s_)], [], "vsta%d" % s_)
            pool(lambda h: h.tensor_copy(out=Va[:nt, va_idx(t), :, 0:64],
                                         in_=vstage[:nt, s_, 0:128].rearrange("p (h d) -> p h d", h=2)),
                 [B("vst%d" % s_), B("Va_ones")], [B("Va%d" % va_idx(t))])
            pool(lambda h: h.tensor_copy(out=Vb[:nt, t, 0:128], in_=vstage[:nt, s_, 128:256]),
                 [B("vst%d" % s_), B("Vb_ones")], [B("Vb%d" % t)])
            yield

        def epilogue_A(t, nt):
            s_ = S2(t)
            g_ = t % 4
            Gt = G if t < NPT else Gs
            OGt = OG if t < NPT else OGs
            e_ = epi[:nt, s_, :]
            ab6 = abank(t)
            accA = accAv(ab6)
            dve(lambda h: h.reciprocal(out=e_[:, 0:2], in_=accA[:nt, :, 64]), [], [B("epiA%d" % s_), BK(ab6)])
            dve(lambda h: h.tensor_scalar(out=e_[:, 2:4], in0=e_[:, 0:2], scalar1=hr_all[:nt, t:t + 1], scalar2=None,
                                          op0=ALU.mult), [B("epiA%d" % s_), B("hr%d" % t)], [B("epiA2%d" % s_)])
            for hh in range(2):
                dve(lambda h, hh=hh: h.scalar_tensor_tensor(out=OGt[:nt, g_, hh * 64:(hh + 1) * 64],
                                                            in0=accA[:nt, hh, 0:64], scalar=e_[:, 2 + hh:3 + hh],
                                                            in1=Gt[:nt, g_, hh * 64:(hh + 1) * 64],
                                                            op0=ALU.mult, op1=ALU.mult),
                    [B("epiA2%d" % s_), B("G%d_%d" % (t >= NPT, g_))],
                    [B("OGa%d_%d" % (t >= NPT, g_)), BK(ab6)])

        def gen_A(t):
            s_ = S2(t)
            ab6 = abank(t)
            accA = accAv(ab6)
            ndj = min(t, 4)
            bq = B("QA%d" % s_)
            if ndj > 0:
                for hh in range(2):
                    for dj in range(ndj, 0, -1):
                        j = t - dj
                        mm(ps[:, ab6, (dj - 1) * 128:dj * 128], K2[:, 0, j * 128:(j + 1) * 128],
                           QA[:, s_, hh, :], dj == ndj, [B("K2_%d" % j), bq], [BK(ab6)])
                    mm(ps[:, ab6, 0:ndj * 128], ident_b[:],
                       EBm[:, hh, 1:1 + ndj, :].rearrange("p r q -> p (r q)"), False,
                       [B("ident_b"), B("EBm")], [BK(ab6)])
                    act(lambda h, hh=hh: h.activation(
                        out=E_A[:, s_, hh, 1:1 + ndj, :].rearrange("p r q -> p (r q)"),
                        in_=ps[:, ab6, 0:ndj * 128], func=AF.Exp, scale=0.125),
                        [], [B("EA%d_%d" % (s_, hh)), BK(ab6)])
                    yield
            for hh in range(2):
                mm(ps[:, ab6, hh * 128:(hh + 1) * 128], K2[:, 0, t * 128:(t + 1) * 128],
                   QA[:, s_, hh, :], hh == 0, [B("K2_%d" % t), bq], [BK(ab6)])
            for hh in range(2):
                mm(ps[:, ab6, hh * 128:(hh + 1) * 128], ident_b[:], EBm[:, hh, 0, :], False,
                   [B("ident_b"), B("EBm")], [BK(ab6)])
            act(lambda h: h.activation(out=E_A[:, s_, :, 0, :],
                                       in_=ps[:, ab6, 0:256].rearrange("p (h q) -> p h q", h=2), func=AF.Exp,
                                       scale=0.125), [], [B("EAd%d_0" % s_), B("EAd%d_1" % s_), BK(ab6)])
            yield
            first = True
            for hh in range(2):
                for dj in range(ndj, -1, -1):
                    j = t - dj
                    mm(accA[:, hh, :], E_A[:, s_, hh, dj, :], Va[:, j % 8, hh, :], first,
                       [B("EA%d_%d" % (s_, hh)), B("EAd%d_%d" % (s_, hh)), B("Va%d" % (j % 8))], [BK(ab6)])
                    first = False
            epilogue_A(t, 128)
            yield

        og_defer = []
        og_round = [0]

        def og_flush(age=4):
            og_round[0] += 1
            while og_defer and (og_round[0] - og_defer[0][2] >= age):
                t_, nt_, _ = og_defer.pop(0)
                og_store(t_, nt_)

        def og_store(t, nt):
            g_ = t % 4
            OGt = OG if t < NPT else OGs
            sfx = "%d_%d" % (t >= NPT, g_)
            ab6 = abank(t)
            og_v = ogv(ab6)
            if t < NPT:
                grp = t // 4
                slot = grp % 2
                c0 = (t % 4) * 128
            else:
                grp = 16
                slot = 0
                c0 = (t - NPT) * 32
            for blk in range(2):
                tr(og_v[:, blk, 0:nt], OGt[:nt, g_, blk * 128:(blk + 1) * 128], ident_b[:nt, :nt],
                   [B("OGa" + sfx), B("OGb" + sfx), B("ident_b")], [BK(ab6)])
            dve(lambda h: h.tensor_copy(out=OGT[:, slot, :, c0:c0 + nt], in_=og_v[:, 0:2, 0:nt]),
                [], [B("OGT%d" % slot), BK(ab6)])
            last = (t % 4 == 3) if t < NPT else (t == NT - 1)
            if last:
                ch, coff = chunk_of(grp)
                ncol = 512 if t < NPT else 128
                dst = o_bounce[ch].ap().rearrange("(b p) t -> p b t", p=128)[:, :, coff:coff + ncol]
                src = OGT[:, slot, :, 0:ncol]
                dma(dst, src, [B("OGT%d" % slot)], [B("o_bounce%d" % ch)], "ogt%d" % slot)
                xs["chunk_cnt"][ch] += 1
                if xs["chunk_cnt"][ch] == (3 if ch < 5 else 2):
                    xs["ag_wait"].append((ch, xs["tile"]))

        def epilogue_B(t, nt, acc_s, sbuf_acc=False):
            s_ = t % 2
            g_ = t % 4
            Gt = G if t < NPT else Gs
            OGt = OG if t < NPT else OGs
            sfx = "%d_%d" % (t >= NPT, g_)
            a0, a1 = acc_s
            e_ = epi[:nt, 2 + s_, :]
            bacc = [B("accS0"), B("accS1")] if sbuf_acc else [BK(2), BK(3)]
            dve(lambda h: h.reciprocal(out=e_[:, 0:1], in_=a0[:nt, 128:129]), ([bacc[0]] if sbuf_acc else []), [B("eB0%d" % s_)] + ([] if sbuf_acc else [bacc[0]]))
            dve(lambda h: h.reciprocal(out=e_[:, 1:2], in_=a1[:nt, 128:129]), ([bacc[1]] if sbuf_acc else []), [B("eB1%d" % s_)] + ([] if sbuf_acc else [bacc[1]]))
            dve(lambda h: h.tensor_scalar(out=e_[:, 2:3], in0=e_[:, 1:2], scalar1=lsc[:nt, 4:5], scalar2=None,
                                          op0=ALU.mult), [B("eB1%d" % s_), B("neglam")], [B("eB2%d" % s_)])
            dve(lambda h: h.tensor_scalar(out=t1[:nt, s_, :], in0=a1[:nt, 0:128], scalar1=e_[:, 2:3], scalar2=None,
                                          op0=ALU.mult), [B("eB2%d" % s_)] + ([bacc[1]] if sbuf_acc else []), [B("t1_%d" % s_)] + ([] if sbuf_acc else [bacc[1]]))
            dve(lambda h: h.scalar_tensor_tensor(out=dd[:nt, s_, :], in0=a0[:nt, 0:128], scalar=e_[:, 0:1],
                                                 in1=t1[:nt, s_, :], op0=ALU.mult, op1=ALU.add),
                [B("eB0%d" % s_), B("t1_%d" % s_)] + ([bacc[0]] if sbuf_acc else []), [B("dd%d" % s_)] + ([] if sbuf_acc else [bacc[0]]))
            act(lambda h: h.activation(out=junk[:nt, 128:256], in_=dd[:nt, s_, :], func=AF.Square,
                                       accum_out=e_[:, 3:4]),
                [B("dd%d" % s_)], [B("eB3%d" % s_), B("junkc")])
            dve(lambda h: h.tensor_scalar(out=e_[:, 4:5], in0=e_[:, 3:4], scalar1=1.0 / 128.0, scalar2=float(EPS),
                                          op0=ALU.mult, op1=ALU.add), [B("eB3%d" % s_)], [B("eB4%d" % s_)])
            pool(lambda h: h.tensor_tensor(out=e_[:, 5:6], in0=e_[:, 4:5], in1=neghalf[:nt, 0:1], op=ALU.pow),
                 [B("eB4%d" % s_), B("neghalf")], [B("eB5%d" % s_)])
            dve(lambda h: h.tensor_scalar(out=e_[:, 6:7], in0=e_[:, 5:6], scalar1=hr_all[:nt, t:t + 1], scalar2=None,
                                          op0=ALU.mult), [B("eB5%d" % s_), B("hr%d" % t)], [B("eB6%d" % s_)])
            dve(lambda h: h.scalar_tensor_tensor(out=OGt[:nt, g_, 128:256], in0=dd[:nt, s_, :], scalar=e_[:, 6:7],
                                                 in1=Gt[:nt, g_, 128:256], op0=ALU.mult, op1=ALU.mult),
                [B("dd%d" % s_), B("eB6%d" % s_), B("G" + sfx)], [B("OGb" + sfx)])
            og_defer.append((t, nt, og_round[0]))

        def gen_B(I):
            qs = I % 2
            nsteps = 2 * I + 2
            bq = [B("QB%d_0" % qs), B("QB%d_1" % qs)]

            SB = (0, 1, 5)

            def scores(j):
                mm(ps[:, SB[j % 3], :], K2[:, 1, j * 128:(j + 1) * 128], QB[:, qs, :], True,
                   [B("K2_%d" % j)] + bq, [BK(SB[j % 3])])

            scores(0)
            if nsteps > 1:
                scores(1)
            for j in range(nsteps):
                if j + 2 < nsteps:
                    scores(j + 2)
                c0 = 0 if j <= 2 * I else 128
                pb = j % 4
                sbk = SB[j % 3]
                bP = B("PB%d" % pb)
                PBf = PB[:, pb, :, :].rearrange("p m q -> p (m q)")
                PBv = PBf[:, 0:512].rearrange("p (m q) -> p m q", m=2)
                if c0 == 0:
                    act(lambda h, sbk=sbk, PBf=PBf: h.activation(out=PBf[:, 0:512], in_=ps[:, sbk, :], func=AF.Exp,
                                                                 scale=0.125), [], [bP, BK(sbk)])
                else:
                    act(lambda h, sbk=sbk, PBv=PBv: h.activation(
                        out=PBv[:, :, 128:256], in_=ps[:, sbk, :].rearrange("p (m q) -> p m q", m=2)[:, :, 128:256],
                        func=AF.Exp, scale=0.125), [], [bP, BK(sbk)])
                if j >= 2 * I:
                    d0 = 0 if j == 2 * I else 128
                    pool(lambda h, PBv=PBv, d0=d0: h.memset(PBv[64:128, :, d0:d0 + 64], 0.0), [bP], [bP])
                for m in range(2):
                    for s in range(2):
                        if s * 128 < c0:
                            continue
                        mm(accB[m][:, s, :], PBf[:, m * 256 + s * 128:m * 256 + (s + 1) * 128], Vb[:, j, :],
                           (j == 0 and s == 0), [bP, B("Vb%d" % j)], [BK(2 + m)])
                yield

        def B_evac(I):
            for m in range(2):
                dve(lambda h, m=m: h.tensor_copy(out=accS[:, m, :, :], in_=accB[m][:, :, :]), [],
                    [B("accS%d" % m), BK(2 + m)])

        def B_epilogues(I):
            for s in range(2):
                epilogue_B(2 * I + s, 128, (accS[:, 0, s, :], accS[:, 1, s, :]), sbuf_acc=True)

        def cache_key_loads(bb):
            if bb >= 4:
                return
            sl = bb % 2
            dma(ck16a[:, sl, :, :], c_ak.ap()[bb].rearrange("(k p) c -> p k c", p=128), [], [B("ck16a%d" % sl)],
                "ck16a%d" % sl, eng="pool")
            dma(ck16b[:, sl, :, :], c_bk.ap()[bb].rearrange("(k p) c -> p k c", p=128), [], [B("ck16b%d" % sl)],
                "ck16b%d" % sl, eng="pool")

        def sample_cache(bb):
            sl = bb % 2
            pb4 = pbank(NPT + bb)
            tp_v = tpv(pb4)
            if bb == 0:
                cache_key_loads(0)
            for hh in range(2):
                dma(CVA[:, :, hh, 0:64],
                    c_av.ap()[bb].rearrange("(k p) (h d) -> p k h d", p=128, h=2)[:, :, hh, :], [B("CVA_ones")],
                    [B("CVA%d" % hh)], "cva%d" % hh, eng="pool")
            dma(CVB[:, :, 0:128], c_bv.ap()[bb].rearrange("(k p) c -> p k c", p=128), [B("CVB_ones")], [B("CVB")],
                "cvb", eng="pool")
            if DO_ROLL:
                dma(aks_out.ap()[bb, 0:480, :], c_ak.ap()[bb, 32:512, :], [], [], "roll")
                dma(avs_out.ap()[bb, 0:480, :], c_av.ap()[bb, 32:512, :], [], [], "roll")
            for k in range(4):
                tr(tp_v[:, k, :], ck16a[:, sl, k, :], ident_b[:], [B("ck16a%d" % sl), B("ident_b")], [BK(pb4)])
            dve(lambda h: h.tensor_copy(out=CKA[:].rearrange("p (k t) -> p k t", k=4), in_=tp_v[:, 0:4, :]),
                [], [B("CKA"), BK(pb4)])
            yield
            for k in range(8):
                tr(tp_v[:, k, :], ck16b[:, sl, k, :], ident_b[:], [B("ck16b%d" % sl), B("ident_b")], [BK(pb4)])
            dve(lambda h: h.tensor_copy(out=CKB[:].rearrange("p (k t) -> p k t", k=8), in_=tp_v[:, :, :]),
                [], [B("CKB"), BK(pb4)])
            cache_key_loads(bb + 1)
            yield

        def gen_SA(t):
            nt, r0 = tile_geo(t)
            s_ = S2(t)
            ab6 = abank(t)
            accA = accAv(ab6)
            bq = B("QA%d" % s_)
            bEl = [B("EA%d_0" % s_), B("EA%d_1" % s_), B("EAd%d_0" % s_), B("EAd%d_1" % s_)]
            first = True
            for hh in range(2):
                for dj in range(4, 0, -1):
                    kt = 4 - dj
                    c0 = (dj - 1) * 128 + hh * 32
                    mm(ps[:, ab6, c0:c0 + 32], CKA[:, kt * 128:(kt + 1) * 128],
                       QA[:, s_, hh, 0:32], first, [B("CKA"), bq], [BK(ab6)])
                    first = False
                mm(ps[0:32, ab6, 64 + hh * 32:96 + hh * 32], K2[:, 0, r0:r0 + 32], QA[:, s_, hh, 0:32], False,
                   [B("K2_%d" % t), bq], [BK(ab6)])
            for hh in range(2):
                for dj in range(4, 0, -1):
                    c0 = (dj - 1) * 128 + hh * 32
                    mm(ps[:, ab6, c0:c0 + 32], ident_b[:], EBu[:, hh, dj, 0:32], False,
                       [B("ident_b"), B("EBu")], [BK(ab6)])
                mm(ps[0:32, ab6, 64 + hh * 32:96 + hh * 32], ident_b[0:32, 0:32], EBu[0:32, hh, 0, 0:32], False,
                   [B("ident_b"), B("EBu")], [BK(ab6)])
            for hh in range(2):
                act(lambda h, hh=hh: h.activation(
                    out=E_A[:, s_, hh, 1:5, 0:32],
                    in_=ps[:, ab6, :].rearrange("p (r q) -> p r q", r=4)[:, :, hh * 32:hh * 32 + 32], func=AF.Exp,
                    scale=0.125), [], bEl + [BK(ab6)])
                act(lambda h, hh=hh: h.activation(out=E_A[0:32, s_, hh, 0, 0:32],
                                                  in_=ps[0:32, ab6, 64 + hh * 32:96 + hh * 32],
                                                  func=AF.Exp, scale=0.125), [], bEl + [BK(ab6)])
            yield
            first = True
            for hh in range(2):
                for dj in range(4, 0, -1):
                    kt = 4 - dj
                    mm(accA[0:32, hh, :], E_A[:, s_, hh, dj, 0:32], CVA[:, kt, hh, :], first,
                       bEl + [B("CVA0"), B("CVA1")], [BK(ab6)])
                    first = False
                mm(accA[0:32, hh, :], E_A[0:32, s_, hh, 0, 0:32], Va[0:32, va_idx(t), hh, :], False,
                   bEl + [B("Va%d" % va_idx(t))], [BK(ab6)])
            epilogue_A(t, 32)
            yield

        def sample_attn_B(t):
            bb = t - NPT
            nt, r0 = tile_geo(t)
            for m in range(2):
                ms = slice(m * 64, (m + 1) * 64)
                for k in range(8):
                    mm(ps[:, m, k * 32:(k + 1) * 32], CKB[ms, k * 128:(k + 1) * 128], QBs[ms, bb, :], True,
                       [B("CKB"), B("QBs%d" % bb)], [BK(m)])
                mm(ps[0:32, m, 256:288], K2[ms, 1, r0:r0 + 32], QBs[ms, bb, :], True,
                   [B("K2_%d" % t), B("QBs%d" % bb)], [BK(m)])
            bP = B("PB0")
            act(lambda h: h.activation(out=PB[:, 0, :, 0:256], in_=ps[:, 0:2, 0:256], func=AF.Exp, scale=0.125),
                [], [bP, BK(0), BK(1)])
            act(lambda h: h.activation(out=PB[0:32, 0, :, 256:288], in_=ps[0:32, 0:2, 256:288], func=AF.Exp,
                                       scale=0.125), [], [bP, BK(0), BK(1)])
            for m in range(2):
                for k in range(8):
                    mm(accB[m][0:32, 0, :], PB[:, 0, m, k * 32:(k + 1) * 32], CVB[:, k, :], k == 0,
                       [bP, B("CVB")], [BK(2 + m)])
                mm(accB[m][0:32, 0, :], PB[0:32, 0, m, 256:288], Vb[0:32, t, :], False,
                   [bP, B("Vb%d" % t)], [BK(2 + m)])
            epilogue_B(t, 32, (accB[0][:, 0, :], accB[1][:, 0, :]))

        xs = {"chunk_cnt": [0] * 6, "ag_wait": [], "p2_wait": [], "p2_gens": [], "tile": 0, "fin_wait": [],
              "fin_cnt": 0}

        def groups_of(ch):
            return [3 * ch, 3 * ch + 1, 3 * ch + 2] if ch < 5 else [15, 16]

        def p2_load_oT(u):
            us = u % 2
            ntok = 512 if u < 16 else 128
            ch, coff = chunk_of(u)
            o_all_v = o_all[ch].ap().rearrange("(c p) t -> p c t", p=128)
            dma(oT[:, us, :, 0:ntok], o_all_v[:, :, coff:coff + ntok], [B("o_all%d" % ch)], [B("oT%d" % us)],
                "oT%d" % us)

        def p2_load_xr(ti):
            ys = ti % 3
            dma(xr[:, ys, :], xres.ap()[ti * 128:ti * 128 + 128, :], [], [B("xr%d" % ys)], "xr%d" % ys)

        def gen_chunk(ch):
            grp = groups_of(ch)
            tis = []
            for u in grp:
                tis += [(u, sub) for sub in range(4 if u < 16 else 1)]
            p2_load_oT(grp[0])
            p2_load_xr(tis[0][0] * 4 + tis[0][1])
            yield
            yield
            for i_, (u, sub) in enumerate(tis):
                us = u % 2
                ti = u * 4 + sub
                r0 = ti * 128
                ys = ti % 3
                pb_ = 6 + (ti % 2)
                if i_ + 1 < len(tis):
                    un, subn = tis[i_ + 1]
                    p2_load_xr(un * 4 + subn)
                    if subn == 0:
                        p2_load_oT(un)
                for cc in range(8):
                    mm(ps[:, pb_, 0:256], oT[:, us, cc, sub * 128:(sub + 1) * 128], Wo[:, cc, :], cc == 0,
                       [B("oT%d" % us), B("Wo")], [BK(pb_)])
                dve(lambda h, ys=ys, pb_=pb_: h.tensor_tensor(out=yt[:, ys, :], in0=ps[:, pb_, 0:256], in1=xr[:, ys, :],
                                                              op=ALU.add),
                    [B("xr%d" % ys)], [B("yt%d" % ys), BK(pb_)])
                act(lambda h, ys=ys, ti=ti: h.activation(out=xsq[:, 0:256], in_=yt[:, ys, :], func=AF.Square,
                                                         accum_out=ssq[:, ti:ti + 1]),
                    [B("yt%d" % ys)], [B("xsq"), B("ssq")])
                dma(y_scr.ap()[r0:r0 + 128, :], yt[:, ys, :], [B("yt%d" % ys)], [B("y_scr%d" % ti)], "yt%d" % ys)
                yield
            t0_ = 12 * ch
            n_ = CH_TILES[ch]
            dma(ss_b[ch].ap(), ssq[:, t0_:t0_ + n_], [B("ssq")], [B("ss_b%d" % ch)], "ssb%d" % ch)
            P.op("pool", lambda h, ch=ch: h.collective_compute(
                "AllReduce", ALU.add, replica_groups=[[0, 1, 2, 3], [4, 5, 6, 7]],
                ins=[ss_b[ch].ap().opt()], outs=[ss_a[ch].ap().opt()]),
                 reads=[B("ss_b%d" % ch)], writes=[B("ss_a%d" % ch)], akey="ar%d" % ch, aamt=1)
            xs["fin_wait"].append((ch, xs["tile"]))
            yield

        def gen_FIN(ch):
            t0_ = 12 * ch
            n_ = CH_TILES[ch]
            brs = B("rstdf%d" % ch)
            grp = groups_of(ch)

            def geo(u):
                nsub = 4 if u < 16 else 1
                k = u % 2
                return nsub, k, fst[:, k, 0:nsub * 256].rearrange("p (s c) -> p s c", s=nsub), [B("fst%d" % k)]

            def fload(u):
                nsub, k, stg, bst = geo(u)
                r0 = u * 512
                dma(stg, y_scr.ap()[r0:r0 + nsub * 128, :].rearrange("(s p) c -> p s c", p=128),
                    [B("y_scr%d" % (u * 4 + i)) for i in range(nsub)], bst, "fstl%d" % k)

            dma(ssr[:, t0_:t0_ + n_], ss_a[ch].ap(), [B("ss_a%d" % ch)], [B("ssr%d" % ch)], "ssr%d" % ch)
            fload(grp[0])
            yield
            dve(lambda h: h.tensor_scalar(out=ssr[:, t0_:t0_ + n_], in0=ssr[:, t0_:t0_ + n_], scalar1=1.0 / 1024.0,
                                          scalar2=float(EPS), op0=ALU.mult, op1=ALU.add),
                [B("ssr%d" % ch)], [B("ssr%d" % ch)])
            pool(lambda h: h.tensor_tensor(out=rstdf[:, t0_:t0_ + n_], in0=ssr[:, t0_:t0_ + n_],
                                           in1=neghalf[:, 0:n_], op=ALU.pow), [B("ssr%d" % ch), B("neghalf")], [brs])
            yield
            for i_, u in enumerate(grp):
                nsub, k, stg, bst = geo(u)
                r0 = u * 512
                if i_ + 1 < len(grp):
                    fload(grp[i_ + 1])
                for sub in range(nsub):
                    ti = u * 4 + sub
                    dve(lambda h, stg=stg, sub=sub, ti=ti: h.scalar_tensor_tensor(
                        out=stg[:, sub, :], in0=stg[:, sub, :], scalar=rstdf[:, ti:ti + 1], in1=fgb[:],
                        op0=ALU.mult, op1=ALU.mult), [brs, B("fgb")], bst)
                dma(y_out.ap()[r0:r0 + nsub * 128, :].rearrange("(s p) c -> p s c", p=128), stg, bst, [],
                    "fsts%d" % k)
                yield

        def exchange_tick(flush=False):
            if not (DO_P2 and DO_CC):
                return
            for item in list(xs["ag_wait"]):
                ch, t0_ = item
                if flush or xs["tile"] - t0_ >= 2:
                    xs["ag_wait"].remove(item)
                    P.op("pool", lambda h, ch=ch: h.collective_compute(
                        "AllGather", ALU.bypass, replica_groups=[[0, 1, 2, 3], [4, 5, 6, 7]],
                        ins=[o_bounce[ch].ap().opt()], outs=[o_all[ch].ap().opt()]),
                         reads=[B("o_bounce%d" % ch)], writes=[B("o_all%d" % ch)], akey="ag%d" % ch, aamt=1)
                    xs["p2_wait"].append((ch, xs["tile"]))
            for item in list(xs["p2_wait"]):
                ch, t0_ = item
                if flush or xs["tile"] - t0_ >= 4:
                    xs["p2_wait"].remove(item)
                    xs["p2_gens"].append(gen_chunk(ch))
            for item in list(xs["fin_wait"]):
                ch, t0_ = item
                if flush or xs["tile"] - t0_ >= 4:
                    xs["fin_wait"].remove(item)
                    xs["p2_gens"].append(gen_FIN(ch))

        def p2_step(n):
            for _ in range(n):
                while xs["p2_gens"]:
                    g = step(xs["p2_gens"][0])
                    if g is None:
                        xs["p2_gens"].pop(0)
                        continue
                    break

        def step(g):
            if g is None:
                return None
            try:
                next(g)
                return g
            except StopIteration:
                return None

        def run_all(g):
            while g is not None:
                g = step(g)

        pool(lambda h: h.memset(QA[64:128, :, 0, :], 0.0), [], [B("QA0"), B("QA1")])
        pool(lambda h: h.memset(QA[0:64, :, 1, :], 0.0), [], [B("QA0"), B("QA1")])
        pool(lambda h: h.memset(QB[64:128, :, 0:256], 0.0), [], [B("QB0_0"), B("QB0_1"), B("QB1_0"), B("QB1_1")])
        pool(lambda h: h.memset(QB[0:64, :, 256:512], 0.0), [], [B("QB0_0"), B("QB0_1"), B("QB1_0"), B("QB1_1")])

        def interleave(gens, fill=True):
            gens = list(gens)
            while any(g is not None for g in gens):
                for i_, g_ in enumerate(gens):
                    gens[i_] = step(g_)
                if fill:
                    p2_step(2 if len(xs["p2_gens"]) > 1 else 1)
                    og_flush()

        def run_fill(g, every=5, early=None):
            n_ = 0
            while g is not None:
                g = step(g)
                n_ += 1
                if n_ == 1 and early is not None:
                    early()
                    early = None
                if n_ % every == 0:
                    p2_step(1)
                    og_flush()
            if early is not None:
                early()

        pairs = [tiles[k:k + 2] for k in range(0, len(tiles), 2)]
        for t_ in pairs[0]:
            load_x(t_)
        for t_ in pairs[0]:
            prep(t_)
            xT(t_)
        pend_epi = []
        for k, pr in enumerate(pairs):
            xs["tile"] = 2 * k
            exchange_tick()
            nxt_pr = pairs[k + 1] if k + 1 < len(pairs) else []
            for t_ in nxt_pr:
                load_x(t_)
            interleave([gen_P(t_) for t_ in pr], fill=False)
            xs["tile"] = 2 * k + 1
            exchange_tick()
            if pr[0] < NPT:
                interleave([gen_A(t_) for t_ in pr], fill=False)
                for t_ in nxt_pr:
                    prep(t_)
                    xT(t_)
                I_ = pr[0] // 2
                prev = list(pend_epi)
                pend_epi = []

                def early_fn(prev=prev):
                    for J_ in prev:
                        B_epilogues(J_)

                run_fill(gen_B(I_), early=early_fn)
                og_flush(age=-10 ** 6)
                B_evac(I_)
                pend_epi.append(I_)
            else:
                for t_ in pr:
                    run_all(sample_cache(t_ - NPT))
                    run_all(gen_SA(t_))
                    sample_attn_B(t_)
                    p2_step(1)
                    og_flush()
                for t_ in nxt_pr:
                    prep(t_)
                    xT(t_)
        for I_ in pend_epi:
            B_epilogues(I_)
        og_flush(age=-10 ** 6)
        if DO_P2:
            for _ in range(4):
                exchange_tick(flush=True)
                while xs["p2_gens"]:
                    p2_step(1)

        with nc.Block() as block:
            P.emit(nc, block, st)
    return nc


_NC_CACHE = {}


def _rope_table():
    half = 8
    inv = (np.float32(500000.0) ** (-np.arange(0, 16, 2, dtype=np.float32) / np.float32(16))).astype(np.float32)
    pos = np.concatenate([np.arange(S, dtype=np.float32)] + [1024 + np.arange(32, dtype=np.float32)] * 4)
    ang = (pos[:, None] * inv[None, :]).astype(np.float32)
    cos = np.cos(ang).astype(np.float32)
    sin = np.sin(ang).astype(np.float32)
    tab = np.zeros((NTOK, 2, 4, half), np.float32)
    tab[:, 0] = cos[:, None, :]
    tab[:, 1] = sin[:, None, :]
    return np.ascontiguousarray(tab.reshape(NTOK, 64))


def kernel(x_prompt, x_sample, cache_a_k, cache_a_v, cache_b_k, cache_b_v,
           norm_gain, w_in, w_out, rel_bias, lambda_q1, lambda_k1, lambda_q2, lambda_k2,
           subln_gain, final_gain):
    f = lambda a: np.ascontiguousarray(np.asarray(a, dtype=np.float32))
    x_prompt, x_sample = f(x_prompt), f(x_sample)
    cache_a_k, cache_a_v, cache_b_k, cache_b_v = f(cache_a_k), f(cache_a_v), f(cache_b_k), f(cache_b_v)
    norm_gain, w_in, w_out, rel_bias = f(norm_gain), f(w_in), f(w_out), f(rel_bias)
    subln_gain, final_gain = f(subln_gain), f(final_gain)
    lam4 = np.concatenate([f(lambda_q1)[0], f(lambda_k1)[0], f(lambda_q2)[0], f(lambda_k2)[0]])[None, :]

    if "nc" not in _NC_CACHE:
        _NC_CACHE["nc"] = build_nc()
    nc = _NC_CACHE["nc"]

    cst = _rope_table()
    ident = np.eye(128, dtype=np.float32)
    jmat = np.ascontiguousarray(ident[::-1])
    W = w_in[0]
    gain_t = np.ascontiguousarray(norm_gain[0].reshape(8, 128).T)
    in_maps = []
    for c in range(8):
        b, g = divmod(c, 4)
        sl = slice(128 * g, 128 * g + 128)
        cols = np.concatenate([np.arange(0, 128) + 128 * g + off for off in
                               (0, 512, 2048, 2560, 1024, 3072, 1536, 3584)])
        x_all = np.concatenate([x_prompt[b]] + [x_sample[4 * b + bb] for bb in range(4)], axis=0)
        rows = []
        for r in range(4):
            rows.append(np.arange(128) + 128 * r)
            rows.append(np.arange(128) + 512 + 128 * r)
        rows = np.concatenate(rows)
        w_out_g = w_out[0][rows][:, 256 * g:256 * g + 256].reshape(8, 128, 256)
        m = {
            "x_all": np.ascontiguousarray(x_all),
            "w_in": np.ascontiguousarray(W[:, cols].reshape(8, 128, 1024)),
            "gain_t": gain_t,
            "w_out": np.ascontiguousarray(w_out_g),
            "sgain": np.ascontiguousarray(subln_gain[0].reshape(128, 1)),
            "xres": np.ascontiguousarray(x_all[:, 256 * g:256 * g + 256]),
            "fgain": np.ascontiguousarray(final_gain[256 * g:256 * g + 256].reshape(1, 256)),
            "relb": np.ascontiguousarray(rel_bias[0, 2 * g:2 * g + 2]),
            "lam4": np.ascontiguousarray(lam4),
            "cst": cst,
            "c_ak": np.ascontiguousarray(cache_a_k[0, 4 * b:4 * b + 4, :, 2 * g:2 * g + 2, :].reshape(4, 512, 128)),
            "c_av": np.ascontiguousarray(cache_a_v[0, 4 * b:4 * b + 4, :, 2 * g:2 * g + 2, :].reshape(4, 512, 128)),
            "c_bk": np.ascontiguousarray(cache_b_k[0, 4 * b:4 * b + 4, :, g].reshape(4, 1024, 128)),
            "c_bv": np.ascontiguousarray(cache_b_v[0, 4 * b:4 * b + 4, :, g].reshape(4, 1024, 128)),
            "ident": ident,
            "jmat": jmat,
        }
        in_maps.append(m)

    res = run_bass_kernel_spmd(nc, in_maps, core_ids=list(range(8)))
    R = res.results

    y_prompt = np.empty((2, S, 1024), np.float32)
    y_sample = np.empty((8, 32, 1024), np.float32)
    akp = np.empty((1, 2, 512, 8, 64), np.float32)
    avp = np.empty((1, 2, 512, 8, 64), np.float32)
    bkp = np.empty((1, 2, S, 4, 2, 64), np.float32)
    bvp = np.empty((1, 2, S, 4, 128), np.float32)
    aks = np.empty((1, 8, 512, 8, 64), np.float32)
    avs = np.empty((1, 8, 512, 8, 64), np.float32)
    bks = np.empty((1, 8, 32, 4, 2, 64), np.float32)
    bvs = np.empty((1, 8, 32, 4, 128), np.float32)
    for c in range(8):
        b, g = divmod(c, 4)
        r = R[c]
        y = np.asarray(r["y_out"])
        y_prompt[b, :, 256 * g:256 * g + 256] = y[:S]
        y_sample[4 * b:4 * b + 4, :, 256 * g:256 * g + 256] = y[S:].reshape(4, 32, 256)
        kb = np.asarray(r["kb_out"])
        vb = np.asarray(r["vb_out"])
        bkp[0, b, :, g] = kb[:S].reshape(S, 2, 64)
        bvp[0, b, :, g] = vb[:S]
        bks[0, 4 * b:4 * b + 4, :, g] = kb[S:].reshape(4, 32, 2, 64)
        bvs[0, 4 * b:4 * b + 4, :, g] = vb[S:].reshape(4, 32, 128)
        akp[0, b, :, 2 * g:2 * g + 2] = np.asarray(r["akp_out"]).reshape(512, 2, 64)
        avp[0, b, :, 2 * g:2 * g + 2] = np.asarray(r["avp_out"]).reshape(512, 2, 64)
        aks[0, 4 * b:4 * b + 4, :, 2 * g:2 * g + 2] = np.asarray(r["aks_out"]).reshape(4, 512, 2, 64)
        avs[0, 4 * b:4 * b + 4, :, 2 * g:2 * g + 2] = np.asarray(r["avs_out"]).reshape(4, 512, 2, 64)
    return (y_prompt, y_sample, akp, avp, bkp, bvp, aks, avs, bks, bvs)
```
